# Optimizing a Trainium2 kernel written in Bass

```python
import jax, jax.numpy as jnp
from jax import lax
import numpy as np

D_MODEL = 1024
BATCH = 8
SEQ = 2048
DEPTH = 2

N_MIXERS = 2
N_LAYERS_A = (DEPTH + 1) // 2
N_LAYERS_B = DEPTH // 2
D_CONV = D_MODEL
CONV_WIDTH = 31
HEAD_DIM = 64
N_HEADS_B = D_MODEL // HEAD_DIM
D_ATTN = N_HEADS_B * HEAD_DIM
DILATED_GROUPS = ((128, 1), (512, 4), (2048, 16))
N_GROUPS = len(DILATED_GROUPS)
BLOCK = 128
IN_COLS_B = N_GROUPS * 3 * D_ATTN + D_ATTN
NORM_EPS = 1e-6
NEG_INF = -1e30

kernel_name = "hybrid_conv_dilated_attn_adaln"


def rms_norm(x, g):
    xf = x.astype(jnp.float32)
    y = xf * lax.rsqrt(jnp.mean(xf * xf, axis=-1, keepdims=True) + NORM_EPS)
    return (y * g.astype(jnp.float32)).astype(x.dtype)


def layer_norm(x, g, b):
    xf = x.astype(jnp.float32)
    mu = jnp.mean(xf, axis=-1, keepdims=True)
    xc = xf - mu
    y = xc * lax.rsqrt(jnp.mean(xc * xc, axis=-1, keepdims=True) + NORM_EPS)
    return (y * g.astype(jnp.float32) + b.astype(jnp.float32)).astype(x.dtype)


def alibi_slopes(n_heads):
    return jnp.exp2(-8.0 * jnp.arange(1, n_heads + 1, dtype=jnp.float32) / n_heads)


def ada_modulation(c, w, b):
    mod = jax.nn.silu(c) @ w + b
    shift, scale, gate = jnp.split(mod, 3, axis=-1)
    return shift[:, None, :], scale[:, None, :], gate[:, None, :]


def conformer_conv_mixer(h, w_in, conv_w, conv_b, ln_g, ln_b, w_out):
    proj = h @ w_in
    val, glu_gate, z = jnp.split(proj, 3, axis=-1)
    u = val * jax.nn.sigmoid(glu_gate)
    u = lax.conv_general_dilated(
        u, conv_w[:, None, :], window_strides=(1,), padding=[(CONV_WIDTH - 1, 0)],
        dimension_numbers=("NWC", "WIO", "NWC"), feature_group_count=D_CONV) + conv_b
    u = jax.nn.silu(layer_norm(u, ln_g, ln_b))
    return (u * jax.nn.silu(z)) @ w_out


def dilated_window_group(q, k, v, window, dilation, slopes):
    B, S, H, Dh = q.shape
    n_steps = window // dilation
    L = S // dilation
    nb = -(-L // BLOCK)
    Lp = nb * BLOCK
    N = B * dilation

    def to_classes(t):
        t = t.reshape(B, L, dilation, H, Dh).transpose(0, 2, 1, 3, 4).reshape(N, L, H, Dh)
        return jnp.pad(t, ((0, 0), (0, Lp - L), (0, 0), (0, 0)))

    def band(t):
        t = jnp.pad(t, ((0, 0), (BLOCK, 0), (0, 0), (0, 0))).reshape(N, nb + 1, BLOCK, H, Dh)
        return jnp.concatenate([t[:, :-1], t[:, 1:]], axis=2)

    qb = to_classes(q).reshape(N, nb, BLOCK, H, Dh)
    kb = band(to_classes(k))
    vb = band(to_classes(v))

    s = jnp.einsum("nbqhd,nbkhd->nhbqk", qb, kb) * (Dh ** -0.5)
    qi = jnp.arange(BLOCK)[:, None]
    kj = jnp.arange(2 * BLOCK)[None, :]
    steps = qi + BLOCK - kj
    key_idx = jnp.arange(nb)[:, None, None] * BLOCK + kj[None] - BLOCK
    valid = (steps >= 0) & (steps <= n_steps) & (key_idx >= 0)
    dist = (steps * dilation).astype(jnp.float32)
    s = s - slopes[:, None, None, None] * dist
    s = jnp.where(valid, s, NEG_INF)
    lse = jax.nn.logsumexp(s, axis=-1)
    p = jnp.exp(s - lse[..., None])
    o = jnp.einsum("nhbqk,nbkhd->nbqhd", p, vb)

    def from_classes(t):
        t = t.reshape((B, dilation, Lp) + t.shape[3:])[:, :, :L]
        return jnp.moveaxis(t, 1, 2).reshape((B, S) + t.shape[3:])

    return from_classes(o), from_classes(jnp.moveaxis(lse, 1, -1))


def dilated_attention_mixer(h, w_in, q_norm, k_norm, w_out):
    B, S, _ = h.shape
    proj = h @ w_in
    qkv = proj[..., :N_GROUPS * 3 * D_ATTN].reshape(B, S, N_GROUPS, 3, N_HEADS_B, HEAD_DIM)
    z = proj[..., N_GROUPS * 3 * D_ATTN:]
    slopes = alibi_slopes(N_HEADS_B)
    outs, lses = [], []
    for g, (window, dilation) in enumerate(DILATED_GROUPS):
        q = rms_norm(qkv[:, :, g, 0], q_norm[g]).astype(jnp.float32)
        k = rms_norm(qkv[:, :, g, 1], k_norm[g]).astype(jnp.float32)
        v = qkv[:, :, g, 2].astype(jnp.float32)
        o, lse = dilated_window_group(q, k, v, window, dilation, slopes)
        outs.append(o)
        lses.append(lse)
    wts = jax.nn.softmax(jnp.stack(lses), axis=0)
    o = jnp.sum(wts[..., None] * jnp.stack(outs), axis=0)
    o = o.reshape(B, S, D_ATTN).astype(h.dtype)
    return (o * jax.nn.silu(z)) @ w_out


def setup_inputs(seed: int = 0) -> dict:
    key = jax.random.key(seed)
    ks = jax.random.split(key, 16)
    f32 = jnp.float32
    nrm = lambda k, shape, s: jax.random.normal(k, shape, f32) * s
    return {
        "x": nrm(ks[0], (BATCH, SEQ, D_MODEL), 1.0),
        "c": nrm(ks[1], (BATCH, D_MODEL), 1.0),
        "norm_g": 1.0 + nrm(ks[2], (DEPTH, D_MODEL), 0.05),
        "ada_w": nrm(ks[3], (DEPTH, D_MODEL, 3 * D_MODEL), D_MODEL ** -0.5),
        "ada_b": nrm(ks[4], (DEPTH, 3 * D_MODEL), 0.02),
        "a_w_in": nrm(ks[5], (N_LAYERS_A, D_MODEL, 3 * D_CONV), D_MODEL ** -0.5),
        "a_conv_w": nrm(ks[6], (N_LAYERS_A, CONV_WIDTH, D_CONV), CONV_WIDTH ** -0.5),
        "a_conv_b": nrm(ks[7], (N_LAYERS_A, D_CONV), 0.02),
        "a_ln_g": 1.0 + nrm(ks[8], (N_LAYERS_A, D_CONV), 0.05),
        "a_ln_b": nrm(ks[9], (N_LAYERS_A, D_CONV), 0.02),
        "a_w_out": nrm(ks[10], (N_LAYERS_A, D_CONV, D_MODEL), D_CONV ** -0.5),
        "b_w_in": nrm(ks[11], (N_LAYERS_B, D_MODEL, IN_COLS_B), D_MODEL ** -0.5),
        "b_q_norm": 1.0 + nrm(ks[12], (N_LAYERS_B, N_GROUPS, HEAD_DIM), 0.05),
        "b_k_norm": 1.0 + nrm(ks[13], (N_LAYERS_B, N_GROUPS, HEAD_DIM), 0.05),
        "b_w_out": nrm(ks[14], (N_LAYERS_B, D_ATTN, D_MODEL), D_ATTN ** -0.5),
    }


def reference(x, c, norm_g, ada_w, ada_b, a_w_in, a_conv_w, a_conv_b, a_ln_g, a_ln_b, a_w_out,
              b_w_in, b_q_norm, b_k_norm, b_w_out):
    for layer in range(DEPTH):
        shift, scale, gate = ada_modulation(c, ada_w[layer], ada_b[layer])
        h = rms_norm(x, norm_g[layer]) * (1.0 + scale) + shift
        j = layer // N_MIXERS
        if layer % N_MIXERS == 0:
            y = conformer_conv_mixer(h, a_w_in[j], a_conv_w[j], a_conv_b[j], a_ln_g[j], a_ln_b[j], a_w_out[j])
        else:
            y = dilated_attention_mixer(h, b_w_in[j], b_q_norm[j], b_k_norm[j], b_w_out[j])
        x = x + gate * y
    return x
```

```python
import numpy as np
import concourse.bass as bass
import concourse.mybir as mybir
from concourse.bass_utils import run_bass_kernel_spmd

F32 = mybir.dt.float32
BF16 = mybir.dt.bfloat16
AF = mybir.ActivationFunctionType
ALU = mybir.AluOpType

S = 2048
D = 1024
NT = 16
DILS = (1, 4, 16)
EPS = 1e-6
KW = 31
NVEC = 72 + 8 * KW + 6
V_NG = (0, 8)
V_SHB = (16, 32)
V_SCB = (24, 40)
V_CONVB, V_LNG, V_LNB = 48, 56, 64
V_CONVW = 72
V_QG = 72 + 8 * KW
V_KG = V_QG + 3


class Prog:
    def __init__(self, nc):
        self.nc = nc
        self.names = ["pe", "act", "dve", "pool", "sp"]
        self.ops = {e: [] for e in self.names}
        self.sem = {e: nc.alloc_semaphore("s_" + e) for e in self.names}
        self.cnt = {e: 0 for e in self.names}
        self.waited = {e: {} for e in self.names}
        self.lw = {}
        self.rd = {}
        self.dsem = {}
        self.dcnt = {}

    def _handle(self, key):
        return self.sem[key[1]] if key[0] == "e" else self.dsem[key[1]]

    def _collect(self, reads, writes):
        need = {}

        def add(tok):
            k, v = tok
            if need.get(k, 0) < v:
                need[k] = v

        for r in reads:
            if r in self.lw:
                add(self.lw[r])
        for w in writes:
            if w in self.lw:
                add(self.lw[w])
            for tok in self.rd.get(w, {}).items():
                add(tok)
        return need

    def _waits(self, eng, need):
        for k, v in need.items():
            if k == ("e", "pe") and eng == "pe":
                continue
            if self.waited[eng].get(k, 0) >= v:
                continue
            self.waited[eng][k] = v
            h = self._handle(k)
            self.ops[eng].append(lambda e, h=h, v=v: e.wait_ge(h, v))

    def _update(self, reads, writes, tok):
        k, v = tok
        for r in reads:
            d = self.rd.setdefault(r, {})
            if d.get(k, 0) < v:
                d[k] = v
        for w in writes:
            self.lw[w] = tok
            self.rd[w] = {}

    def op(self, eng, fn, reads=(), writes=()):
        need = self._collect(reads, writes)
        self._waits(eng, need)
        self.cnt[eng] += 1
        s = self.sem[eng]
        self.ops[eng].append(lambda e, fn=fn, s=s: fn(e).then_inc(s, 1))
        self._update(reads, writes, (("e", eng), self.cnt[eng]))

    def dma(self, eng, sname, out, in_, reads=(), writes=()):
        if sname not in self.dsem:
            self.dsem[sname] = self.nc.alloc_semaphore("d_" + sname)
            self.dcnt[sname] = 0
        need = self._collect(reads, writes)
        self._waits(eng, need)
        self.dcnt[sname] += 16
        h = self.dsem[sname]
        self.ops[eng].append(lambda e, h=h, out=out, in_=in_: e.dma_start(out=out, in_=in_).then_inc(h, 16))
        self._update(reads, writes, (("d", sname), self.dcnt[sname]))

    def wait_all(self, eng, regions):
        self._waits(eng, self._collect(regions, ()))

    def barrier(self):
        need = {("e", e): self.cnt[e] for e in self.names if self.cnt[e] > 0}
        for sname, c in self.dcnt.items():
            need[("d", sname)] = c
        for e in self.names:
            if e == "pe":
                continue
            self._waits(e, dict(need))

    def emit(self):
        nc = self.nc
        ops = self.ops
        with nc.Block() as block:
            @block.tensor
            def _(e):
                for f in ops["pe"]:
                    f(e)

            @block.scalar
            def _(e):
                for f in ops["act"]:
                    f(e)

            @block.vector
            def _(e):
                for f in ops["dve"]:
                    f(e)

            @block.gpsimd
            def _(e):
                for f in ops["pool"]:
                    f(e)

            @block.sync
            def _(e):
                for f in ops["sp"]:
                    f(e)


class Arena:
    def __init__(self, nc, nbytes):
        self.nbytes = nbytes
        self.t = nc.alloc_sbuf_tensor("arena", [128, nbytes // 4], F32)
        self.tb = self.t.bitcast(BF16)
        self.off = 0
        self.peak = 0

    def reset(self, off=0):
        self.off = off

    def alloc(self, shape, dtype):
        n = 1
        for s_ in shape:
            n *= s_
        esz = 4 if dtype == F32 else 2
        nb = (n * esz + 31) // 32 * 32
        o = self.off
        self.off += nb
        self.peak = max(self.peak, self.off)
        assert self.off <= getattr(self, "limit", self.nbytes), f"arena overflow {self.off} > {getattr(self, 'limit', self.nbytes)}"
        if dtype == F32:
            ap = self.t[:, o // 4: o // 4 + n]
        else:
            ap = self.tb[:, o // 2: o // 2 + n]
        if len(shape) == 2:
            return ap.rearrange("p (a b) -> p a b", a=shape[0])
        if len(shape) == 3:
            return ap.rearrange("p (a b c) -> p a b c", a=shape[0], b=shape[1])
        return ap


def qblock_dst(g, qb):
    if g == 0:
        return [(qb // 4, (qb % 4) * 128, 1, 128, 0)]
    if g == 1:
        r, b = qb // 4, qb % 4
        return [(b, r, 4, 128, 0)]
    return [(j, qb, 16, 32, 32 * j) for j in range(4)]


STOP = None
VARIANT = None


class _Stop(Exception):
    pass


def build(layers, first, last):
    nc = bass.Bass("TRN2", target_bir_lowering=False, dynamic_dma_scratch_size=4096)
    P = Prog(nc)

    def dram(name, shape, kind="ExternalInput"):
        return nc.dram_tensor(name, list(shape), F32, kind=kind).ap()

    x_d = dram("x", [S, D])
    cT_d = dram("cT", [128, 8])
    vecs_d = dram("vecs", [128, NVEC])
    gateb_d = dram("gateb", [1, 2 * D])
    adaw_d = dram("ada_w", [2, D, 3 * D])
    w0_d = dram("w0", [8, D, 384])
    wo0_d = dram("wo0", [D, D])
    w1_d = dram("w1", [8, D, 1280])
    wo1_d = dram("wo1", [D, D])
    emask_d = dram("emask", [8, 128, 6 * 256])
    cst_d = dram("cst", [128, 256])
    y_d = dram("y", [S, D], kind="ExternalOutput")
    x_t = x_d.rearrange("(t p) d -> t p d", p=128)
    y_t = y_d.rearrange("(t p) d -> t p d", p=128)

    xs = nc.alloc_sbuf_tensor("xs", [128, NT, D], F32)
    hT = nc.alloc_sbuf_tensor("hT", [128, 8, S], BF16)
    cst = nc.alloc_sbuf_tensor("cstb", [128, 256], BF16)
    ident = cst[:, 0:128]
    bdones = cst[:, 128:256]
    ones_bf = nc.alloc_sbuf_tensor("ones_bf", [128, 128], BF16)
    ones_f = nc.alloc_sbuf_tensor("ones_f", [128, 128], F32)
    vecs = nc.alloc_sbuf_tensor("vecs_s", [128, NVEC], F32)
    cT = nc.alloc_sbuf_tensor("cTs", [128, 8], F32)
    sc2 = nc.alloc_sbuf_tensor("sc2", [128, 8, 2], F32)
    modT = nc.alloc_sbuf_tensor("modT", [128, 2, 16], F32)
    modA = nc.alloc_sbuf_tensor("modA", [128, 2, 8], F32)
    gate_rep = nc.alloc_sbuf_tensor("gate_rep", [128, 2, D], F32)
    ss = nc.alloc_sbuf_tensor("ss", [128, NT], F32)
    rs = nc.alloc_sbuf_tensor("rs", [128, NT], F32)
    gq8 = nc.alloc_sbuf_tensor("gq8", [128, 3], F32)
    epsb = nc.alloc_sbuf_tensor("epsb", [128, 1], F32)
    banks = [nc.alloc_psum_tensor(f"bank{i}", [128, 512], F32) for i in range(8)]
    bankb = [b.bitcast(BF16) for b in banks]
    BK = [f"bank{i}" for i in range(8)]

    arena = Arena(nc, (nc.sbuf_bytes_remaining - 64) // 32 * 32)
    TOP = (arena.nbytes - (6144 + 2048 + 3072)) // 32 * 32
    arena.reset(TOP)
    top_gsl0 = arena.alloc([8, 384], BF16)
    top_zsl = arena.alloc([8, 128], BF16)
    top_em0 = arena.alloc([6, 256], BF16)
    arena.peak = 0
    arena.limit = TOP
    l1_prefetched = []

    def l1_prefetch0():
        w1h = w1_d[0].rearrange("(c p) n -> p c n", p=128)
        P.dma("pool", "emask0", top_em0[:, :, :], emask_d[0].rearrange("p (a b) -> p a b", a=6), writes=["Em0"])
        P.dma("pool", "w1z", top_zsl[:, :, :], w1h[:, :, 1152:1280], writes=["w1z"])
        P.dma("pool", "w1s0", top_gsl0[:, :, :], w1h[:, :, 0:384], writes=["w1s0"])
        l1_prefetched.append(True)

    P.dma("pool", "cst", cst[:, :], cst_d, writes=["cst"])
    P.dma("sp", "vecsd", vecs[:, :], vecs_d, writes=["vecs"])
    P.dma("sp", "cTd", cT[:, :], cT_d, writes=["cT"])
    P.op("pool", lambda e: e.memset(ones_bf[:, :], 1.0), writes=["ones_bf"])
    P.op("pool", lambda e: e.memset(ones_f[:, :], 1.0), writes=["ones_f"])
    P.op("pool", lambda e: e.memset(epsb[:, :], EPS), writes=["epsb"])
    if first:
        for tt in range(NT):
            P.dma("sp", f"x{tt}", xs[:, tt, :], x_t[tt], writes=[f"x{tt}"])

    arena.reset()
    xn_setup = [arena.alloc([4, D], BF16) for _ in range(2)]
    screp = arena.alloc([8, 128], BF16)
    sc2h = arena.alloc([8, 2], BF16)
    adaslot = [arena.alloc([8, D], BF16) for _ in range(4)]
    rowb = arena.alloc([D], F32)
    P.op("act", lambda e: e.activation(out=sc2[:, :, 0], in_=cT[:, :], func=AF.Silu), reads=["cT"], writes=["sc2a"])
    P.op("act", lambda e: e.activation(out=sc2[:, :, 1], in_=cT[:, :], func=AF.Silu), reads=["cT"], writes=["sc2b"])
    P.op("dve", lambda e: e.tensor_copy(out=sc2h[:, :, :], in_=sc2[:, :, :]), reads=["sc2a", "sc2b"], writes=["sc2h"])
    for dc in range(8):
        P.op("dve", lambda e, dc=dc: e.tensor_scalar(out=screp[:, dc, :], in0=ones_f[:, :], scalar1=sc2[:, dc, 0:1],
                                                      scalar2=None, op0=ALU.mult),
             reads=["ones_f", "sc2a"], writes=[f"screp{dc}"])
    P.op("dve", lambda e: e.scalar_tensor_tensor(out=gq8[:, :], in0=vecs[:, V_QG:V_QG + 3], scalar=0.125,
                                                 in1=vecs[:, V_KG:V_KG + 3], op0=ALU.mult, op1=ALU.mult),
         reads=["vecs"], writes=["gq8"])
    piece_c = [0]
    ada_slots = {}

    def ada_dma(l, part):
        if (l, part) in ada_slots:
            return ada_slots[(l, part)]
        sl = piece_c[0] % 4
        piece_c[0] += 1
        P.dma("pool", f"ada{sl}", adaslot[sl][:, :, :],
              adaw_d[l].rearrange("(c p) n -> p c n", p=128)[:, :, part * D:(part + 1) * D], writes=[f"adaslot{sl}"])
        ada_slots[(l, part)] = sl
        return sl

    ada_dma(layers[0], 0)
    ada_dma(layers[0], 1)
    ada_dma(layers[0], 2)
    if len(layers) > 1:
        ada_dma(layers[1], 0)

    def emit_mod(l):
            adaw_l = adaw_d[l].rearrange("(c p) n -> p c n", p=128)
            for part in range(3):
                sl = ada_dma(l, part)
                if part == 2:
                    for half in range(2):
                        P.dma("sp", f"gbias{l}_{half}", gate_rep[:, l, half * 512:(half + 1) * 512],
                              gateb_d[0:1, l * D + half * 512:l * D + (half + 1) * 512].broadcast_to([128, 512]),
                              writes=[f"gate_rep{l}_{half}"])
                if part < 2:
                    for half in range(2):
                        def mmr(e, sl=sl, half=half):
                            r = None
                            for dc in range(8):
                                r = e.matmul(banks[1 + half][0:2, :], lhsT=sc2h[:, dc, :],
                                             rhs=adaslot[sl][:, dc, half * 512:(half + 1) * 512], start=(dc == 0), stop=(dc == 7))
                            return r
                        P.op("pe", mmr, reads=[f"adaslot{sl}", "sc2h"], writes=[BK[1 + half]])
                        P.op("act", lambda e, half=half: e.activation(out=rowb[0:1, half * 512:(half + 1) * 512],
                                                                      in_=banks[1 + half][0:1, :], func=AF.Copy),
                             reads=[BK[1 + half]], writes=[f"rowb{half}"])

                    def mm(e):
                        r = None
                        for j in range(8):
                            r = e.matmul(banks[0][:, 2 * j:2 * j + 2], lhsT=rowb[0:1, j * 128:(j + 1) * 128],
                                         rhs=ones_f[0:1, 0:2], start=True, stop=True)
                        return r
                    P.op("pe", mm, reads=["rowb0", "rowb1", "ones_f"], writes=[BK[0]])
                    P.op("dve", lambda e, l=l, part=part: e.tensor_tensor(
                        out=modT[:, l, part * 8:(part + 1) * 8], in0=banks[0][:, 0:16:2],
                        in1=vecs[:, (V_SHB, V_SCB)[part][l]:(V_SHB, V_SCB)[part][l] + 8], op=ALU.add),
                        reads=[BK[0], "vecs"], writes=[f"modT{l}_{part}"])
                else:
                    for half in range(2):
                        def mm(e, sl=sl, half=half, l=l):
                            r = None
                            for dc in range(8):
                                r = e.matmul(banks[1 + half][:, :], lhsT=screp[:, dc, :],
                                             rhs=adaslot[sl][:, dc, half * 512:(half + 1) * 512], start=(dc == 0), stop=(dc == 7))
                            return r
                        P.op("pe", mm, reads=[f"adaslot{sl}"] + [f"screp{dc}" for dc in range(8)], writes=[BK[1 + half]])
                        P.op("dve", lambda e, l=l, half=half: e.tensor_tensor(
                            out=gate_rep[:, l, half * 512:(half + 1) * 512], in0=banks[1 + half][:, :],
                            in1=gate_rep[:, l, half * 512:(half + 1) * 512], op=ALU.add),
                            reads=[BK[1 + half], f"gate_rep{l}_{half}"], writes=[f"gate_rep{l}_{half}"])
            P.op("dve", lambda e, l=l: e.scalar_tensor_tensor(
                out=modA[:, l, :], in0=modT[:, l, 8:16], scalar=1.0, in1=vecs[:, V_NG[l]:V_NG[l] + 8],
                op0=ALU.add, op1=ALU.mult), reads=[f"modT{l}_1", "vecs"], writes=[f"modA{l}"])

    EARLY_NORM = (layers[0] == 0)
    stop_at = STOP
    norm_done = set()

    def maybe_stop(tag):
        if stop_at == tag:
            for tt in range(NT):
                P.dma("sp", f"x{tt}", y_t[tt], xs[:, tt, :], reads=[f"x{tt}"])
            raise _Stop()

    def norm_E(l, tg, xnb, xk, extra_w=(), xn_engs=("pool", "dve")):
        for j in range(4):
            tt = tg * 4 + j
            P.op("act", lambda e, tt=tt, j=j: e.activation(out=xnb[:, j, :], in_=xs[:, tt, :], func=AF.Square,
                                                           accum_out=ss[:, tt:tt + 1]),
                 reads=[f"x{tt}"], writes=[f"{xk}_{j}", f"ss{tt}"] + list(extra_w))
        sreg = [f"ss{tg * 4 + j}" for j in range(4)]
        P.op("act", lambda e: e.activation(out=ss[:, tg * 4:tg * 4 + 4], in_=ss[:, tg * 4:tg * 4 + 4], func=AF.Sqrt,
                                           scale=1.0 / D, bias=epsb[:, 0:1]), reads=sreg + ["epsb"], writes=sreg)
        P.op("dve", lambda e: e.reciprocal(out=rs[:, tg * 4:tg * 4 + 4], in_=ss[:, tg * 4:tg * 4 + 4]),
             reads=sreg, writes=[f"rs{tg}"])
        for j in range(4):
            tt = tg * 4 + j
            eng = xn_engs[j % 2]
            P.op(eng, lambda e, tt=tt, j=j: e.tensor_scalar(
                out=xnb[:, j, :], in0=xs[:, tt, :], scalar1=rs[:, tt:tt + 1], scalar2=1.0,
                op0=ALU.mult, op1=ALU.mult), reads=[f"x{tt}", f"rs{tg}"], writes=[f"{xk}_{j}"])

    def norm_T(l, tg, xnb, xk, all_act=False):
        for c in range(8):
            pb = 6 + (c % 2)

            def tr(e, c=c, pb=pb):
                r = None
                for j in range(4):
                    r = e.transpose(out=bankb[pb][:, j * 128:(j + 1) * 128], in_=xnb[:, j, c * 128:(c + 1) * 128],
                                    identity=ident)
                return r
            P.op("pe", tr, reads=[f"{xk}_{j}" for j in range(4)] + ["cst"], writes=[BK[pb]])
            if c % 2 == 0 or all_act:
                P.op("act", lambda e, c=c, pb=pb: e.activation(
                    out=hT[:, c, tg * 512:(tg + 1) * 512], in_=bankb[pb][:, 0:512], func=AF.Identity,
                    scale=modA[:, l, c:c + 1], bias=modT[:, l, c:c + 1]),
                    reads=[BK[pb], f"modA{l}", f"modT{l}_0"], writes=[f"hT{c}_{tg}"])
            else:
                P.op("dve", lambda e, c=c, pb=pb: e.tensor_scalar(
                    out=hT[:, c, tg * 512:(tg + 1) * 512], in0=bankb[pb][:, 0:512],
                    scalar1=modA[:, l, c:c + 1], scalar2=modT[:, l, c:c + 1], op0=ALU.mult, op1=ALU.add),
                    reads=[BK[pb], f"modA{l}", f"modT{l}_0"], writes=[f"hT{c}_{tg}"])

    def phase_norm(l, xn=None, barrier=True, mod_first=None):
        if xn is None:
            arena.reset()
            xn = [arena.alloc([4, D], BF16) for _ in range(2)]
        norm_E(l, 0, xn[0], "xn0")
        if mod_first is not None:
            norm_E(l, 1, xn[1], "xn1")
            emit_mod(mod_first)
        for tg in range(4):
            if tg + 1 < 4 and not (mod_first is not None and tg == 0):
                norm_E(l, tg + 1, xn[(tg + 1) % 2], f"xn{(tg + 1) % 2}")
            norm_T(l, tg, xn[tg % 2], f"xn{tg % 2}")
        if barrier:
            P.barrier()

    def hT_regs(tiles):
        return [f"hT{dc}_{tg}" for dc in range(8) for tg in tiles]

    def phase_out(l, wo_d, uT, ureg, is_last):
        stg = [arena.alloc([D], F32) for _ in range(8)]
        wog = arena.alloc([8, D], BF16)
        for kc in range(8):
            P.dma("sp", f"wostgL{l}_{kc}", stg[kc][:, :], wo_d[kc * 128:(kc + 1) * 128, :], writes=[f"wostgL{kc}"])
        for kc in range(8):
            P.op("dve" if kc % 2 == 0 else "pool", lambda e, kc=kc: e.tensor_tensor(
                out=wog[:, kc, :], in0=stg[kc][:, :], in1=gate_rep[:, l, :], op=ALU.mult),
                reads=[f"wostgL{kc}", f"gate_rep{l}_0", f"gate_rep{l}_1"], writes=[f"wog{kc}"])
        maybe_stop("O1")
        n = 0
        for tt in range(NT):
            for half in range(2):
                pb = 4 + n % 4
                n += 1

                def mm(e, tt=tt, half=half, pb=pb):
                    r = None
                    for kc in range(8):
                        r = e.matmul(banks[pb][:, :], lhsT=uT[:, kc, tt * 128:(tt + 1) * 128],
                                     rhs=wog[:, kc, half * 512:(half + 1) * 512], start=(kc == 0), stop=(kc == 7))
                    return r
                P.op("pe", mm, reads=[f"wog{kc}" for kc in range(8)] + ureg(tt), writes=[BK[pb]])
                P.op("dve", lambda e, tt=tt, half=half, pb=pb: e.tensor_tensor(
                    out=xs[:, tt, half * 512:(half + 1) * 512], in0=banks[pb][:, :],
                    in1=xs[:, tt, half * 512:(half + 1) * 512], op=ALU.add),
                    reads=[BK[pb], f"x{tt}"], writes=[f"x{tt}"])
            if is_last and stop_at != "O2":
                P.dma("sp", f"x{tt}", y_t[tt], xs[:, tt, :], reads=[f"x{tt}"])
        maybe_stop("O2")
        P.barrier()

    def layer0():
        l = 0
        phase_norm(l, xn_setup, barrier=False, mod_first=0)
        for l2 in layers[1:]:
            emit_mod(l2)
        P.barrier()
        maybe_stop("norm")
        arena.reset()
        cTb = arena.alloc([8, S], BF16)
        S1 = arena.alloc([S], F32)
        S2 = arena.alloc([S], F32)
        zsl = [arena.alloc([8, 128], BF16) for _ in range(8)]
        mark = arena.off
        wsl = [arena.alloc([8, 256], BF16) for _ in range(2)]
        diag = [arena.alloc([KW, 128], BF16) for _ in range(2)]
        ub = [arena.alloc([S + 32], BF16) for _ in range(2)]
        sig = [arena.alloc([512], F32) for _ in range(2)]
        csq = [arena.alloc([512], BF16) for _ in range(2)]
        for i in range(2):
            P.op("pool", lambda e, i=i: e.memset(ub[i][:, 0:32], 0.0), writes=[f"upad{i}"])

        def load_w(cc):
            sl = cc % 2
            w0c = w0_d[cc].rearrange("(c p) n -> p c n", p=128)
            P.dma("pool", f"w0s{sl}", wsl[sl][:, :, :], w0c[:, :, 0:256], writes=[f"w0s{sl}"])

        def make_diag(cc):
            sl = cc % 2
            for k in range(KW):
                P.op("pool", lambda e, k=k, sl=sl, cc=cc: e.tensor_scalar(
                    out=diag[sl][:, k, :], in0=ident, scalar1=vecs[:, V_CONVW + cc * KW + k:V_CONVW + cc * KW + k + 1],
                    scalar2=1.0, op0=ALU.mult, op1=ALU.mult), reads=["cst", "vecs"], writes=[f"diag{sl}_{k}"])

        def stats(cc, t4):
            P.op("pe", lambda e, cc=cc, t4=t4: e.matmul(banks[6][:, :], lhsT=ones_bf[:, :],
                                                        rhs=cTb[:, cc, t4 * 512:(t4 + 1) * 512], start=True, stop=True),
                 reads=[f"c{cc}_{t4}", "ones_bf"], writes=[BK[6]])
            P.op("pe", lambda e, t4=t4: e.matmul(banks[7][:, :], lhsT=ones_bf[:, :], rhs=csq[t4 % 2][:, :],
                                                 start=True, stop=True),
                 reads=[f"csq{t4 % 2}", "ones_bf"], writes=[BK[7]])
            for (Sx, pb, nm) in ((S1, 6, "S1"), (S2, 7, "S2")):
                if cc == 0:
                    P.op("dve", lambda e, Sx=Sx, pb=pb, t4=t4: e.tensor_copy(out=Sx[:, t4 * 512:(t4 + 1) * 512],
                                                                             in_=banks[pb][:, :]),
                         reads=[BK[pb]], writes=[f"{nm}_{t4}"])
                else:
                    P.op("dve", lambda e, Sx=Sx, pb=pb, t4=t4: e.tensor_tensor(
                        out=Sx[:, t4 * 512:(t4 + 1) * 512], in0=banks[pb][:, :], in1=Sx[:, t4 * 512:(t4 + 1) * 512],
                        op=ALU.add), reads=[BK[pb], f"{nm}_{t4}"], writes=[f"{nm}_{t4}"])

        load_w(0)
        load_w(1)
        make_diag(0)
        pend = None
        for cc in range(8):
            sl = cc % 2
            if cc + 1 < 8:
                make_diag(cc + 1)
            for t4 in range(4):
                pa, pg = 0 + (t4 % 2), 2 + (t4 % 2)

                def mm(e, sl=sl, t4=t4, pa=pa, pg=pg):
                    r = None
                    for dc in range(8):
                        r = e.matmul(banks[pa][:, :], lhsT=wsl[sl][:, dc, 0:128], rhs=hT[:, dc, t4 * 512:(t4 + 1) * 512],
                                     start=(dc == 0), stop=(dc == 7))
                    for dc in range(8):
                        r = e.matmul(banks[pg][:, :], lhsT=wsl[sl][:, dc, 128:256], rhs=hT[:, dc, t4 * 512:(t4 + 1) * 512],
                                     start=(dc == 0), stop=(dc == 7))
                    return r
                P.op("pe", mm, reads=[f"w0s{sl}"] + hT_regs([t4]), writes=[BK[pa], BK[pg]])
                if pend is not None and t4 == 0:
                    stats(*pend)
                    pend = None
                P.op("act", lambda e, t4=t4, pg=pg: e.activation(out=sig[t4 % 2][:, :], in_=banks[pg][:, :], func=AF.Sigmoid),
                     reads=[BK[pg]], writes=[f"sig{t4 % 2}"])
                P.op("dve", lambda e, t4=t4, pa=pa, sl=sl: e.tensor_tensor(
                    out=ub[sl][:, 32 + t4 * 512:32 + (t4 + 1) * 512], in0=banks[pa][:, :], in1=sig[t4 % 2][:, :], op=ALU.mult),
                    reads=[BK[pa], f"sig{t4 % 2}"], writes=[f"u{sl}_{t4}"])
            if cc + 2 < 8:
                load_w(cc + 2)
            P.dma("pool", f"w0z{cc}", zsl[cc][:, :, :], w0_d[cc].rearrange("(c p) n -> p c n", p=128)[:, :, 256:384],
                  writes=[f"w0z{cc}"])
            for t4 in range(4):
                pc = 4 + (t4 % 2)

                def mmc(e, sl=sl, t4=t4, pc=pc):
                    r = None
                    for k in range(KW):
                        o = 2 + t4 * 512 + k
                        r = e.matmul(banks[pc][:, :], lhsT=diag[sl][:, k, :], rhs=ub[sl][:, o:o + 512],
                                     start=(k == 0), stop=(k == KW - 1))
                    return r
                ur = [f"u{sl}_{t4}"] + ([f"u{sl}_{t4 - 1}"] if t4 > 0 else [f"upad{sl}"])
                P.op("pe", mmc, reads=ur + [f"diag{sl}_{k}" for k in range(KW)], writes=[BK[pc]])
                if pend is not None:
                    stats(*pend)
                P.op("act", lambda e, cc=cc, t4=t4, pc=pc: e.activation(
                    out=cTb[:, cc, t4 * 512:(t4 + 1) * 512], in_=banks[pc][:, :], func=AF.Identity,
                    bias=vecs[:, V_CONVB + cc:V_CONVB + cc + 1], scale=1.0),
                    reads=[BK[pc], "vecs"], writes=[f"c{cc}_{t4}"])
                P.op("act", lambda e, cc=cc, t4=t4, pc=pc: e.activation(
                    out=csq[t4 % 2][:, :], in_=banks[pc][:, :], func=AF.Square,
                    bias=vecs[:, V_CONVB + cc:V_CONVB + cc + 1], scale=1.0),
                    reads=[BK[pc], "vecs"], writes=[f"csq{t4 % 2}"])
                pend = (cc, t4)
        stats(*pend)
        P.barrier()
        maybe_stop("A")
        arena.reset(mark)
        t1 = [arena.alloc([512], F32) for _ in range(2)]
        t2 = [arena.alloc([512], F32) for _ in range(2)]
        szt = [arena.alloc([512], BF16) for _ in range(2)]
        stg_all = arena.alloc([2, D], F32)
        stg = [stg_all[:, 0, :], stg_all[:, 1, :]]
        xn1 = stg_all.bitcast(BF16).rearrange("p a (b c) -> p (a b) c", c=D)
        wog = arena.alloc([8, D], BF16)
        def wog_dma(kc):
            sl = kc % 2
            P.dma("sp", f"wostg{sl}", stg[sl], wo0_d[kc * 128:(kc + 1) * 128, :], writes=[f"wostg{sl}"])

        def wog_mul(kc):
            sl = kc % 2
            P.op("pool", lambda e, kc=kc, sl=sl: e.tensor_tensor(out=wog[:, kc, :], in0=stg[sl],
                                                                 in1=gate_rep[:, 0, :], op=ALU.mult),
                 reads=[f"wostg{sl}", "gate_rep0_0", "gate_rep0_1"], writes=[f"wog{kc}"])

        def finalize(t4):
            sl_ = slice(t4 * 512, (t4 + 1) * 512)
            P.op("dve", lambda e, sl_=sl_: e.tensor_scalar(out=S1[:, sl_], in0=S1[:, sl_], scalar1=1.0 / D, scalar2=None,
                                                           op0=ALU.mult), reads=[f"S1_{t4}"], writes=[f"S1_{t4}"])
            P.op("dve", lambda e, sl_=sl_: e.tensor_tensor(out=t1[0][:, :], in0=S1[:, sl_], in1=S1[:, sl_], op=ALU.mult),
                 reads=[f"S1_{t4}"], writes=["t1_0"])
            P.op("dve", lambda e, sl_=sl_: e.scalar_tensor_tensor(out=S2[:, sl_], in0=S2[:, sl_], scalar=1.0 / D,
                                                                  in1=t1[0][:, :], op0=ALU.mult, op1=ALU.subtract),
                 reads=[f"S2_{t4}", "t1_0"], writes=[f"S2_{t4}"])
            P.op("act", lambda e, sl_=sl_: e.activation(out=S2[:, sl_], in_=S2[:, sl_], func=AF.Ln, bias=epsb[:, 0:1],
                                                        scale=1.0), reads=[f"S2_{t4}", "epsb"], writes=[f"S2_{t4}"])
            P.op("act", lambda e, sl_=sl_: e.activation(out=S2[:, sl_], in_=S2[:, sl_], func=AF.Exp, scale=-0.5),
                 reads=[f"S2_{t4}"], writes=[f"S2_{t4}"])
            P.op("dve", lambda e, sl_=sl_: e.scalar_tensor_tensor(out=S1[:, sl_], in0=S1[:, sl_], scalar=-1.0,
                                                                  in1=S2[:, sl_], op0=ALU.mult, op1=ALU.mult),
                 reads=[f"S1_{t4}", f"S2_{t4}"], writes=[f"S1_{t4}"])
        cnt0 = {"n": 0, "o": 0}

        def z_tiles(t4, wog0=None):
            sl_ = slice(t4 * 512, (t4 + 1) * 512)
            pend_uu = None
            with_wog = wog0 is not None
            if with_wog:
                wog_dma(wog0)
                wog_dma(wog0 + 1)
            for cc in range(8):
                b = cnt0["n"] % 2
                pz = (cnt0["n"] % 8) if t4 == 0 else (cnt0["n"] % 4)
                cnt0["n"] += 1

                def mm(e, cc=cc, pz=pz):
                    r = None
                    for dc in range(8):
                        r = e.matmul(banks[pz][:, :], lhsT=zsl[cc][:, dc, :], rhs=hT[:, dc, t4 * 512:(t4 + 1) * 512],
                                     start=(dc == 0), stop=(dc == 7))
                    return r
                P.op("pe", mm, reads=[f"w0z{cc}"] + hT_regs([t4]), writes=[BK[pz]])
                P.op("act", lambda e, b=b, pz=pz: e.activation(out=szt[b][:, :], in_=banks[pz][:, :], func=AF.Silu),
                     reads=[BK[pz]], writes=[f"szt{b}"])
                P.op("dve", lambda e, b=b, cc=cc: e.tensor_tensor(out=t1[b][:, :], in0=cTb[:, cc, sl_],
                                                                   in1=S2[:, sl_], op=ALU.mult),
                     reads=[f"c{cc}_{t4}", f"S2_{t4}"], writes=[f"t1_{b}"])
                P.op("dve", lambda e, b=b: e.tensor_tensor(out=t2[b][:, :], in0=t1[b][:, :], in1=S1[:, sl_], op=ALU.add),
                     reads=[f"t1_{b}", f"S1_{t4}"], writes=[f"t2_{b}"])
                P.op("act", lambda e, b=b, cc=cc: e.activation(out=t2[b][:, :], in_=t2[b][:, :], func=AF.Silu,
                                                               scale=vecs[:, V_LNG + cc:V_LNG + cc + 1],
                                                               bias=vecs[:, V_LNB + cc:V_LNB + cc + 1]),
                     reads=[f"t2_{b}", "vecs"], writes=[f"t2_{b}"])
                if pend_uu is not None:
                    pend_uu()

                def uu(b=b, cc=cc):
                    P.op("pool", lambda e: e.tensor_tensor(out=cTb[:, cc, sl_], in0=t2[b][:, :], in1=szt[b][:, :], op=ALU.mult),
                         reads=[f"t2_{b}", f"szt{b}"], writes=[f"c{cc}_{t4}"])
                pend_uu = uu
                if with_wog and cc % 2 == 1:
                    kc_ = wog0 + cc // 2
                    wog_mul(kc_)
                    if cc // 2 + 2 < 4:
                        wog_dma(kc_ + 2)
            pend_uu()

        def out_tiles(t4):
            for tt in range(4 * t4, 4 * t4 + 4):
                for half in range(2):
                    pb = 4 + cnt0["o"] % 4
                    cnt0["o"] += 1

                    def mm(e, tt=tt, half=half, pb=pb):
                        r = None
                        for kc in range(8):
                            r = e.matmul(banks[pb][:, :], lhsT=cTb[:, kc, tt * 128:(tt + 1) * 128],
                                         rhs=wog[:, kc, half * 512:(half + 1) * 512], start=(kc == 0), stop=(kc == 7))
                        return r
                    P.op("pe", mm, reads=[f"wog{kc}" for kc in range(8)] + [f"c{kc}_{t4}" for kc in range(8)],
                         writes=[BK[pb]])
                    P.op("dve", lambda e, tt=tt, half=half, pb=pb: e.tensor_tensor(
                        out=xs[:, tt, half * 512:(half + 1) * 512], in0=banks[pb][:, :],
                        in1=xs[:, tt, half * 512:(half + 1) * 512], op=ALU.add),
                        reads=[BK[pb], f"x{tt}"], writes=[f"x{tt}"])
                if last and layers[-1] == 0:
                    P.dma("sp", f"x{tt}", y_t[tt], xs[:, tt, :], reads=[f"x{tt}"])

        fuse_next = (1 in layers)
        if fuse_next:
            l1_prefetch0()
        finalize(0)
        z_tiles(0)
        finalize(1)
        z_tiles(1, wog0=0)
        finalize(2)
        z_tiles(2, wog0=4)
        for t4 in range(4):
            out_tiles(t4)
            if t4 == 0:
                finalize(3)
                z_tiles(3)
            if fuse_next:
                if t4 > 0:
                    norm_T(1, t4 - 1, xn1, "xn1", all_act=True)
                norm_E(1, t4, xn1, "xn1", extra_w=("wostg0", "wostg1"), xn_engs=("pool", "pool"))
        if fuse_next:
            norm_T(1, 3, xn1, "xn1", all_act=True)
            norm_done.add(1)
        maybe_stop("B")
        P.barrier()

    def layer1():
        l = 1
        if 1 not in norm_done:
            phase_norm(l)
        maybe_stop("n1")
        arena.reset()
        uT = arena.alloc([8, S], BF16)
        mark = arena.off
        qk = [[arena.alloc([S], BF16) for _ in range(2)] for _ in range(3)]
        Vt = [arena.alloc([16, 192], BF16) for _ in range(3)]
        Em = [top_em0, arena.alloc([6, 256], BF16)]
        gsl = [top_gsl0, arena.alloc([8, 384], BF16)]
        zsl = top_zsl
        sz = arena.alloc([S], BF16)
        sqt = [arena.alloc([512], BF16) for _ in range(2)]
        sdt = [arena.alloc([512], F32) for _ in range(2)]
        NPT = 4
        PTall = arena.alloc([NPT * 512], BF16)
        PT = [PTall[:, i * 512:(i + 1) * 512] for i in range(NPT)]
        rd = arena.alloc([512], F32)
        ot = arena.alloc([512], F32)
        sq4 = [(sqt[0], "sqt0"), (sqt[1], "sqt1"), (rd.bitcast(BF16)[:, 0:512], "rdA"), (ot.bitcast(BF16)[:, 0:512], "otA")]
        for g in range(3):
            P.op("pool", lambda e, g=g: e.memset(Vt[g][:, :, 64:128], 1.0), writes=[f"Vones{g}"])

        def tok_tile(g, dc, t4):
            if g == 0:
                return hT[:, dc, t4 * 512:(t4 + 1) * 512]
            if g == 1:
                return hT[:, dc, t4::4]
            return hT[:, dc, :].rearrange("p (i r) -> p r i", r=16)[:, 4 * t4:4 * t4 + 4, :]

        def tok_block(g, dc, blk):
            if g == 0:
                return hT[:, dc, blk * 128:(blk + 1) * 128]
            if g == 1:
                r, b = blk // 4, blk % 4
                s0 = 4 * 128 * b + r
                return hT[:, dc, s0:s0 + 509:4]
            return hT[:, dc, blk::16]

        GSLOT = (0, 1, 0)
        PPB = (6, 7, 0, 1, 2, 3)
        all_h = hT_regs(range(4))

        def load_g(hp, g):
            w1h = w1_d[hp].rearrange("(c p) n -> p c n", p=128)
            sl = GSLOT[g]
            P.dma("pool", f"w1s{sl}", gsl[sl][:, :, :], w1h[:, :, g * 384:(g + 1) * 384], writes=[f"w1s{sl}"])

        def prefetch(hp):
            w1h = w1_d[hp].rearrange("(c p) n -> p c n", p=128)
            if hp == 0 and l1_prefetched:
                load_g(hp, 1)
                return
            P.dma("pool", f"emask{hp % 2}", Em[hp % 2][:, :, :], emask_d[hp].rearrange("p (a b) -> p a b", a=6),
                  writes=[f"Em{hp % 2}"])
            P.dma("pool", "w1z", zsl[:, :, :], w1h[:, :, 1152:1280], writes=["w1z"])
            load_g(hp, 0)
            load_g(hp, 1)

        cnt = {"p": 0, "n": 0, "s": 0, "t": 0, "q": 0}

        def proj_items(hp):
            items = []
            for g in range(3):
                sl = GSLOT[g]
                for j in range(2):
                    for t4 in range(4):
                        def front(g=g, sl=sl, j=j, t4=t4):
                            pp = PPB[cnt["p"] % 6]
                            cnt["p"] += 1
                            hr = hT_regs([t4])

                            def mm(e):
                                r = None
                                for dc in range(8):
                                    r = e.matmul(banks[pp][:, :], lhsT=gsl[sl][:, dc, j * 128:(j + 1) * 128],
                                                 rhs=hT[:, dc, t4 * 512:(t4 + 1) * 512], start=(dc == 0), stop=(dc == 7))
                                return r
                            P.op("pe", mm, reads=[f"w1s{sl}"] + hr, writes=[BK[pp]])
                            sqx, sqn = sq4[cnt["q"] % 4]
                            cnt["q"] += 1
                            P.op("act", lambda e: e.activation(out=sqx[:, :], in_=banks[pp][:, :], func=AF.Square),
                                 reads=[BK[pp]], writes=[sqn])
                            return (pp, sqx, sqn)

                        def back(tok, g=g, j=j, t4=t4):
                            pp, sqx, sqn = tok
                            b = cnt["n"] % 2
                            cnt["n"] += 1
                            pn = 4 + b
                            P.op("pe", lambda e: e.matmul(banks[pn][:, :], lhsT=bdones, rhs=sqx[:, :], start=True, stop=True),
                                 reads=[sqn, "cst"], writes=[BK[pn]])
                            P.op("act", lambda e: e.activation(out=sdt[b][:, :], in_=banks[pn][:, :], func=AF.Ln,
                                                               scale=1.0 / 64, bias=epsb[:, 0:1]),
                                 reads=[BK[pn], "epsb"], writes=[f"sdt{b}"])
                            P.op("act", lambda e: e.activation(out=sdt[b][:, :], in_=sdt[b][:, :], func=AF.Exp, scale=-0.5),
                                 reads=[f"sdt{b}"], writes=[f"sdt{b}"])
                            gv = gq8[:, g:g + 1] if j == 0 else vecs[:, V_KG + g:V_KG + g + 1]
                            if g == 0:
                                o_ap, i0_ap, i1_ap = qk[g][j][:, t4 * 512:(t4 + 1) * 512], banks[pp][:, :], sdt[b][:, :]
                                wr = [f"qk{g}_{j}_{t4}"]
                            else:
                                R_ = DILS[g]
                                n_ = 512 // R_
                                o_ap = qk[g][j][:, :].rearrange("p (r i) -> p r i", r=R_)[:, :, t4 * n_:(t4 + 1) * n_]
                                i0_ap = banks[pp][:, :].rearrange("p (i r) -> p r i", r=R_)
                                i1_ap = sdt[b][:, :].rearrange("p (i r) -> p r i", r=R_)
                                wr = [f"qk{g}_{j}_{q_}" for q_ in range(4)]
                            if j == 0:
                                P.op("dve", lambda e: e.scalar_tensor_tensor(
                                    out=o_ap, in0=i0_ap, scalar=gv, in1=i1_ap, op0=ALU.mult, op1=ALU.mult),
                                    reads=[BK[pp], f"sdt{b}", "gq8", "vecs"], writes=wr)
                            else:
                                P.op("dve", lambda e: e.tensor_tensor(out=o_ap, in0=i0_ap, in1=i1_ap, op=ALU.mult),
                                     reads=[BK[pp], f"sdt{b}"], writes=wr)
                        items.append((front, back))
                for t4 in range(4):
                    def frontvt(g=g, sl=sl, t4=t4):
                        pp = PPB[cnt["p"] % 6]
                        cnt["p"] += 1

                        def mmv(e):
                            r = None
                            for dc in range(8):
                                r = e.matmul(banks[pp][:, :], lhsT=gsl[sl][:, dc, 256:384],
                                             rhs=hT[:, dc, t4 * 512:(t4 + 1) * 512], start=(dc == 0), stop=(dc == 7))
                            return r
                        P.op("pe", mmv, reads=[f"w1s{sl}"] + hT_regs([t4]), writes=[BK[pp]])
                        if g == 0 and t4 == 3:
                            load_g(hp, 2)
                        return pp

                    def backvt(pp, g=g, t4=t4):
                        if g == 0:
                            o_ap, i_ap, wr = PTall[:, t4 * 512:(t4 + 1) * 512], banks[pp][:, :], [f"PT{t4}"]
                        else:
                            R_ = DILS[g]
                            n_ = 512 // R_
                            o_ap = PTall[:, 0:2048].rearrange("p (r i) -> p r i", r=R_)[:, :, t4 * n_:(t4 + 1) * n_]
                            i_ap = banks[pp][:, :].rearrange("p (i r) -> p r i", r=R_)
                            wr = [f"PT{q_}" for q_ in range(4)]
                        P.op("dve", lambda e: e.tensor_copy(out=o_ap, in_=i_ap), reads=[BK[pp]], writes=wr)
                    items.append((frontvt, backvt))
                items.append(("flush", None))
                for b4 in range(4):
                    def frontv(g=g, b4=b4):
                        pv = PPB[cnt["p"] % 6]
                        cnt["p"] += 1

                        def trv(e):
                            r = None
                            for i in range(4):
                                blk = b4 * 4 + i
                                r = e.transpose(out=bankb[pv][:, i * 128:(i + 1) * 128], in_=PTall[:, blk * 128:(blk + 1) * 128],
                                                identity=ident)
                            return r
                        P.op("pe", trv, reads=[f"PT{b4}", "cst"], writes=[BK[pv]])
                        return pv

                    def backv(pv, g=g, b4=b4):
                        src = bankb[pv][:, 0:512].rearrange("p (b c) -> p b c", c=128)
                        P.op("dve", lambda e: e.tensor_copy(out=Vt[g][:, b4 * 4:b4 * 4 + 4, 0:64], in_=src[:, :, 0:64]),
                             reads=[BK[pv]], writes=[f"V{g}_{b4}a"])
                        P.op("dve", lambda e: e.tensor_copy(out=Vt[g][:, b4 * 4:b4 * 4 + 4, 128:192], in_=src[:, :, 64:128]),
                             reads=[BK[pv]], writes=[f"V{g}_{b4}b"])
                    items.append((frontv, backv))
            for t4 in range(4):
                def frontz(t4=t4):
                    pp = PPB[cnt["p"] % 6]
                    cnt["p"] += 1

                    def mmz(e):
                        r = None
                        for dc in range(8):
                            r = e.matmul(banks[pp][:, :], lhsT=zsl[:, dc, :], rhs=hT[:, dc, t4 * 512:(t4 + 1) * 512],
                                         start=(dc == 0), stop=(dc == 7))
                        return r
                    P.op("pe", mmz, reads=["w1z"] + hT_regs([t4]), writes=[BK[pp]])
                    return pp

                def backz(pp, t4=t4):
                    P.op("act", lambda e: e.activation(out=sz[:, t4 * 512:(t4 + 1) * 512], in_=banks[pp][:, :], func=AF.Silu),
                         reads=[BK[pp]], writes=[f"sz{t4}"])
                items.append((frontz, backz))
            return items

        def run_items(items, look=1, group=1):
            if group > 1:
                return run_items_grouped(items, group)
            pend = []
            deferred = []
            for (front, back) in items:
                if front == "flush":
                    for bk_, tk_ in pend:
                        bk_(tk_)
                    pend = []
                    continue
                if front == "evac":
                    for bk_, tk_ in pend:
                        bk_(tk_)
                    pend = []
                    deferred.append(back)
                    continue

                tok = front()
                pend.append((back, tok))
                if deferred and len(pend) >= look + 1:
                    for d_ in deferred:
                        d_()
                    deferred.clear()
                if len(pend) > look:
                    bk_, tk_ = pend.pop(0)
                    bk_(tk_)
            for bk_, tk_ in pend:
                bk_(tk_)
            for d_ in deferred:
                d_()

        def run_items_grouped(items, group):
            prev = []
            cur = []
            deferred = []

            def flush_prev():
                for bk_, tk_ in prev:
                    bk_(tk_)
                prev.clear()

            for (front, back) in items:
                if front == "evac":
                    flush_prev()
                    for bk_, tk_ in cur:
                        bk_(tk_)
                    cur.clear()
                    deferred.append(back)
                    continue
                tok = front()
                cur.append((back, tok))
                if len(cur) == group:
                    if deferred:
                        for d_ in deferred:
                            d_()
                        deferred.clear()
                    flush_prev()
                    prev.extend(cur)
                    cur.clear()
            flush_prev()
            for bk_, tk_ in cur:
                bk_(tk_)
            for d_ in deferred:
                d_()

        def attn_items(hp, h):
            p0 = 64 * h
            Emh = Em[hp % 2]
            emr = f"Em{hp % 2}"
            started = [False] * 4
            items = []
            for g in range(3):
                if g == 0:
                    jobs = [(kb, 256 if kb < 15 else 128) for kb in range(16)]
                elif g == 1:
                    jobs = [(kb, 256 if kb % 4 < 3 else 128) for kb in range(16)]
                else:
                    jobs = [(kb, 128) for kb in range(16)]
                packs = []
                cur, used = [], 0
                for kb, ncol in jobs:
                    if used + ncol > 512:
                        packs.append(cur)
                        cur, used = [], 0
                    cur.append((kb, ncol, used))
                    used += ncol
                if cur:
                    packs.append(cur)
                for pack in packs:
                    def front(pack=pack, g=g):
                        ps = 4 + cnt["s"] % 4
                        cnt["s"] += 1
                        pt = cnt["t"] % NPT
                        cnt["t"] += 1
                        tot = pack[-1][2] + pack[-1][1]
                        qregs = set()
                        for kb, ncol, off in pack:
                            qregs.add(f"qk{g}_1_{kb // 4}")
                            qregs.add(f"qk{g}_0_{kb // 4}")
                            qregs.add(f"qk{g}_0_{min(15, kb + 1) // 4}")

                        def mms(e):
                            r = None
                            for kb, ncol, off in pack:
                                r = e.matmul(banks[ps][:, off:off + ncol], lhsT=qk[g][1][p0:p0 + 64, kb * 128:(kb + 1) * 128],
                                             rhs=qk[g][0][p0:p0 + 64, kb * 128:kb * 128 + ncol], start=True, stop=True)
                            return r
                        P.op("pe", mms, reads=sorted(qregs), writes=[BK[ps]])
                        P.op("act", lambda e: e.activation(out=PT[pt][:, 0:tot], in_=banks[ps][:, 0:tot], func=AF.Exp),
                             reads=[BK[ps]], writes=[f"PT{pt}"])
                        runs = []
                        for kb, ncol, off in pack:
                            if runs and runs[-1][1] == ncol and runs[-1][0] + runs[-1][1] * runs[-1][2] == off:
                                runs[-1][2] += 1
                            else:
                                runs.append([off, ncol, 1])
                        for off, ncol, cntk in runs:
                            if cntk == 1:
                                P.op("dve", lambda e, off=off, ncol=ncol: e.tensor_tensor(
                                    out=PT[pt][:, off:off + ncol], in0=PT[pt][:, off:off + ncol],
                                    in1=Emh[:, h * 3 + g, 0:ncol], op=ALU.mult), reads=[f"PT{pt}", emr], writes=[f"PT{pt}"])
                            else:
                                P.op("dve", lambda e, off=off, ncol=ncol, cntk=cntk: e.tensor_tensor(
                                    out=PT[pt][:, off:off + ncol * cntk].rearrange("p (a b) -> p a b", a=cntk),
                                    in0=PT[pt][:, off:off + ncol * cntk].rearrange("p (a b) -> p a b", a=cntk),
                                    in1=Emh[:, h * 3 + g:h * 3 + g + 1, 0:ncol].broadcast_to([128, cntk, ncol]),
                                    op=ALU.mult), reads=[f"PT{pt}", emr], writes=[f"PT{pt}"])
                        return pt

                    def back(pt, pack=pack, g=g):
                        bks = set()
                        for kb, ncol, off in pack:
                            for qi in range(ncol // 128):
                                for (bk, c0, cs, nc_, so) in qblock_dst(g, kb + qi):
                                    bks.add(bk)

                        def mmpv(e):
                            r = None
                            for kb, ncol, off in pack:
                                if g == 0 and ncol == 256 and kb % 4 != 3:
                                    bk = kb // 4
                                    c0 = (kb % 4) * 128
                                    r = e.matmul(banks[bk][:, c0:c0 + 256], lhsT=Vt[g][:, kb, 64 * h:64 * h + 128],
                                                 rhs=PT[pt][:, off:off + 256], start=(not started[bk]), stop=False,
                                                 skip_group_check=True)
                                    started[bk] = True
                                    continue
                                for qi in range(ncol // 128):
                                    for (bk, c0, cs, nc_, so) in qblock_dst(g, kb + qi):
                                        o = off + qi * 128 + so
                                        r = e.matmul(banks[bk][:, c0:c0 + cs * (nc_ - 1) + 1:cs],
                                                     lhsT=Vt[g][:, kb, 64 * h:64 * h + 128],
                                                     rhs=PT[pt][:, o:o + nc_], start=(not started[bk]), stop=False,
                                                     skip_group_check=True)
                                        started[bk] = True
                            return r
                        vregs = sorted({f"V{g}_{kb // 4}a" for kb, _, _ in pack} | {f"V{g}_{kb // 4}b" for kb, _, _ in pack})
                        P.op("pe", mmpv, reads=[f"PT{pt}", f"Vones{g}"] + vregs, writes=[BK[b_] for b_ in sorted(bks)])
                    items.append((front, back))
            return items

        def attn_evac(hp, h):
            nlo, dlo = (0, 64) if h == 0 else (64, 0)
            rdb = [(rd, "rdA"), (sdt[0], "sdt0")]
            otb = [(ot, "otA"), (sdt[1], "sdt1")]
            for bk in range(4):
                cs_ = slice(bk * 512, (bk + 1) * 512)
                rdx, rdn = rdb[bk % 2]
                otx, otn = otb[bk % 2]
                P.op("act", lambda e, bk=bk, rdx=rdx: e.activation(out=rdx[nlo:nlo + 64, :], in_=banks[bk][dlo:dlo + 64, :],
                                                                   func=AF.Ln), reads=[BK[bk]], writes=[rdn])
                P.op("act", lambda e, rdx=rdx: e.activation(out=rdx[nlo:nlo + 64, :], in_=rdx[nlo:nlo + 64, :], func=AF.Exp,
                                                            scale=-1.0), reads=[rdn], writes=[rdn])
                P.op("dve", lambda e, bk=bk, rdx=rdx, otx=otx: e.tensor_tensor(
                    out=otx[nlo:nlo + 64, :], in0=banks[bk][nlo:nlo + 64, :], in1=rdx[nlo:nlo + 64, :], op=ALU.mult),
                    reads=[BK[bk], rdn], writes=[otn])
                P.op("dve", lambda e, cs_=cs_, otx=otx: e.tensor_tensor(
                    out=uT[nlo:nlo + 64, hp, cs_], in0=otx[nlo:nlo + 64, :], in1=sz[nlo:nlo + 64, cs_], op=ALU.mult),
                    reads=[otn, f"sz{bk}"], writes=[f"uT{hp}_{h}_{bk}"])

        prefetch(0)
        for hp in range(8):
            cnt["p"] = 0
            run_items(proj_items(hp), look=1)
            maybe_stop("p1")
            if hp + 1 < 8:
                prefetch(hp + 1)
            its = []
            for h in range(2):
                its += attn_items(hp, h)
                its.append(("evac", (lambda hp=hp, h=h: attn_evac(hp, h))))
            run_items(its, group=2)
            maybe_stop("a2")
        P.barrier()
        arena.reset(mark)
        phase_out(1, wo1_d, uT, lambda tt: [f"uT{kc}_{h}_{tt // 4}" for kc in range(8) for h in range(2)],
                  is_last=(last and layers[-1] == 1))

    if not EARLY_NORM:
        emit_mod(layers[0])
    try:
        maybe_stop("setup")
        for l in layers:
            if l == 0:
                layer0()
            else:
                layer1()
    except _Stop:
        pass
    need = {}
    for tt in range(NT):
        for k, v in P.rd.get(f"x{tt}", {}).items():
            need[k] = max(need.get(k, 0), v)
    P._waits("sp", need)
    P.emit()
    return nc, arena.peak


def _pp(v):
    return np.ascontiguousarray(np.asarray(v, np.float32).reshape(8, 128).T)


def _prep_shared(inp):
    f = lambda k: np.asarray(inp[k], np.float32)
    ada_b = f("ada_b")
    vecs = np.zeros((128, NVEC), np.float32)
    ng = f("norm_g")
    for l in range(2):
        vecs[:, V_NG[l]:V_NG[l] + 8] = _pp(ng[l])
        vecs[:, V_SHB[l]:V_SHB[l] + 8] = _pp(ada_b[l, 0:D])
        vecs[:, V_SCB[l]:V_SCB[l] + 8] = _pp(ada_b[l, D:2 * D])
    vecs[:, V_CONVB:V_CONVB + 8] = _pp(f("a_conv_b")[0])
    vecs[:, V_LNG:V_LNG + 8] = _pp(f("a_ln_g")[0])
    vecs[:, V_LNB:V_LNB + 8] = _pp(f("a_ln_b")[0])
    cw = f("a_conv_w")[0]
    vecs[:, V_CONVW:V_CONVW + 8 * KW] = cw.reshape(KW, 8, 128).transpose(2, 1, 0).reshape(128, 8 * KW)
    qn = f("b_q_norm")[0]
    kn = f("b_k_norm")[0]
    vecs[:, V_QG:V_QG + 3] = np.concatenate([qn.T, qn.T], axis=0)
    vecs[:, V_KG:V_KG + 3] = np.concatenate([kn.T, kn.T], axis=0)
    gateb = np.ascontiguousarray(ada_b[:, 2 * D:3 * D].reshape(1, 2 * D))
    w_in0 = f("a_w_in")[0]
    w0 = np.ascontiguousarray(w_in0.reshape(D, 3, 8, 128).transpose(2, 0, 1, 3).reshape(8, D, 384))
    w_in1 = f("b_w_in")[0]
    qkv = w_in1[:, :9216].reshape(D, 3, 3, 8, 128)
    zz = w_in1[:, 9216:].reshape(D, 8, 128)
    w1 = np.concatenate([qkv.transpose(3, 0, 1, 2, 4).reshape(8, D, 1152), zz.transpose(1, 0, 2)], axis=2)
    w1 = np.ascontiguousarray(w1)
    cst = np.zeros((128, 256), np.float32)
    cst[:, 0:128] = np.eye(128, dtype=np.float32)
    cst[0:64, 128:192] = 1.0
    cst[64:128, 192:256] = 1.0
    r = np.arange(128)[:, None].astype(np.float64)
    c = np.arange(256)[None, :].astype(np.float64)
    steps = c - r
    valid = (steps >= 0) & (steps <= 128)
    emask = np.zeros((8, 128, 6, 256), np.float32)
    for hp in range(8):
        for hl in range(2):
            hh = 2 * hp + hl
            slope = 2.0 ** (-8.0 * (hh + 1) / 16)
            for g in range(3):
                emask[hp, :, hl * 3 + g, :] = np.where(valid, np.exp(-slope * DILS[g] * steps), 0.0)
    return dict(vecs=vecs, gateb=gateb, ada_w=np.ascontiguousarray(f("ada_w")), w0=w0,
                wo0=np.ascontiguousarray(f("a_w_out")[0]), w1=w1, wo1=np.ascontiguousarray(f("b_w_out")[0]),
                emask=np.ascontiguousarray(emask.reshape(8, 128, 6 * 256)), cst=cst)


FUSED = True
_cache = {}


def _get_nc(layers, first, last):
    key = (tuple(layers), first, last)
    if key not in _cache:
        _cache[key] = build(list(layers), first, last)[0]
    return _cache[key]


def _run(layers, xin, shared, c):
    nc = _get_nc(layers, True, True)
    in_maps = []
    for b in range(8):
        m = dict(shared)
        m["x"] = np.ascontiguousarray(xin[b])
        m["cT"] = _pp(c[b])
        in_maps.append(m)
    res = run_bass_kernel_spmd(nc, in_maps, core_ids=list(range(8)))
    return np.stack([res.results[b]["y"] for b in range(8)], axis=0)


def kernel(**inputs):
    x = np.asarray(inputs["x"], np.float32)
    c = np.asarray(inputs["c"], np.float32)
    shared = _prep_shared(inputs)
    if FUSED:
        return _run((0, 1), x, shared, c).astype(np.float32)
    x1 = _run((0,), x, shared, c)
    return _run((1,), x1, shared, c).astype(np.float32)
```

```python
import numpy as np
import concourse.bass as bass
import concourse.mybir as mybir
from concourse.bass_utils import run_bass_kernel_spmd

F32 = mybir.dt.float32
BF16 = mybir.dt.bfloat16
AF = mybir.ActivationFunctionType
ALU = mybir.AluOpType

S = 2048
D = 1024
NT = 16
DILS = (1, 4, 16)
EPS = 1e-6
KW = 31
NVEC = 72 + 8 * KW + 6
V_NG = (0, 8)
V_SHB = (16, 32)
V_SCB = (24, 40)
V_CONVB, V_LNG, V_LNB = 48, 56, 64
V_CONVW = 72
V_QG = 72 + 8 * KW
V_KG = V_QG + 3


class Prog:
    def __init__(self, nc):
        self.nc = nc
        self.names = ["pe", "act", "dve", "pool", "sp"]
        self.ops = {e: [] for e in self.names}
        self.sem = {e: nc.alloc_semaphore("s_" + e) for e in self.names}
        self.cnt = {e: 0 for e in self.names}
        self.waited = {e: {} for e in self.names}
        self.lw = {}
        self.rd = {}
        self.dsem = {}
        self.dcnt = {}

    def _handle(self, key):
        return self.sem[key[1]] if key[0] == "e" else self.dsem[key[1]]

    def _collect(self, reads, writes):
        need = {}

        def add(tok):
            k, v = tok
            if need.get(k, 0) < v:
                need[k] = v

        for r in reads:
            if r in self.lw:
                add(self.lw[r])
        for w in writes:
            if w in self.lw:
                add(self.lw[w])
            for tok in self.rd.get(w, {}).items():
                add(tok)
        return need

    def _waits(self, eng, need):
        for k, v in need.items():
            if k == ("e", "pe") and eng == "pe":
                continue
            if self.waited[eng].get(k, 0) >= v:
                continue
            self.waited[eng][k] = v
            h = self._handle(k)
            self.ops[eng].append(lambda e, h=h, v=v: e.wait_ge(h, v))

    def _update(self, reads, writes, tok):
        k, v = tok
        for r in reads:
            d = self.rd.setdefault(r, {})
            if d.get(k, 0) < v:
                d[k] = v
        for w in writes:
            self.lw[w] = tok
            self.rd[w] = {}

    def op(self, eng, fn, reads=(), writes=()):
        need = self._collect(reads, writes)
        self._waits(eng, need)
        self.cnt[eng] += 1
        s = self.sem[eng]
        self.ops[eng].append(lambda e, fn=fn, s=s: fn(e).then_inc(s, 1))
        self._update(reads, writes, (("e", eng), self.cnt[eng]))

    def dma(self, eng, sname, out, in_, reads=(), writes=()):
        if sname not in self.dsem:
            self.dsem[sname] = self.nc.alloc_semaphore("d_" + sname)
            self.dcnt[sname] = 0
        need = self._collect(reads, writes)
        self._waits(eng, need)
        self.dcnt[sname] += 16
        h = self.dsem[sname]
        self.ops[eng].append(lambda e, h=h, out=out, in_=in_: e.dma_start(out=out, in_=in_).then_inc(h, 16))
        self._update(reads, writes, (("d", sname), self.dcnt[sname]))

    def wait_all(self, eng, regions):
        self._waits(eng, self._collect(regions, ()))

    def barrier(self):
        need = {("e", e): self.cnt[e] for e in self.names if self.cnt[e] > 0}
        for sname, c in self.dcnt.items():
            need[("d", sname)] = c
        for e in self.names:
            if e == "pe":
                continue
            self._waits(e, dict(need))

    def emit(self):
        nc = self.nc
        ops = self.ops
        with nc.Block() as block:
            @block.tensor
            def _(e):
                for f in ops["pe"]:
                    f(e)

            @block.scalar
            def _(e):
                for f in ops["act"]:
                    f(e)

            @block.vector
            def _(e):
                for f in ops["dve"]:
                    f(e)

            @block.gpsimd
            def _(e):
                for f in ops["pool"]:
                    f(e)

            @block.sync
            def _(e):
                for f in ops["sp"]:
                    f(e)


class Arena:
    def __init__(self, nc, nbytes):
        self.nbytes = nbytes
        self.t = nc.alloc_sbuf_tensor("arena", [128, nbytes // 4], F32)
        self.tb = self.t.bitcast(BF16)
        self.off = 0
        self.peak = 0

    def reset(self, off=0):
        self.off = off

    def alloc(self, shape, dtype):
        n = 1
        for s_ in shape:
            n *= s_
        esz = 4 if dtype == F32 else 2
        nb = (n * esz + 31) // 32 * 32
        o = self.off
        self.off += nb
        self.peak = max(self.peak, self.off)
        assert self.off <= getattr(self, "limit", self.nbytes), f"arena overflow {self.off} > {getattr(self, 'limit', self.nbytes)}"
        if dtype == F32:
            ap = self.t[:, o // 4: o // 4 + n]
        else:
            ap = self.tb[:, o // 2: o // 2 + n]
        if len(shape) == 2:
            return ap.rearrange("p (a b) -> p a b", a=shape[0])
        if len(shape) == 3:
            return ap.rearrange("p (a b c) -> p a b c", a=shape[0], b=shape[1])
        return ap


def qblock_dst(g, qb):
    if g == 0:
        return [(qb // 4, (qb % 4) * 128, 1, 128, 0)]
    if g == 1:
        r, b = qb // 4, qb % 4
        return [(b, r, 4, 128, 0)]
    return [(j, qb, 16, 32, 32 * j) for j in range(4)]


STOP = None
VARIANT = None


class _Stop(Exception):
    pass


def build(layers, first, last):
    nc = bass.Bass("TRN2", target_bir_lowering=False, dynamic_dma_scratch_size=4096)
    P = Prog(nc)

    def dram(name, shape, kind="ExternalInput"):
        return nc.dram_tensor(name, list(shape), F32, kind=kind).ap()

    x_d = dram("x", [S, D])
    cT_d = dram("cT", [128, 8])
    vecs_d = dram("vecs", [128, NVEC])
    gateb_d = dram("gateb", [1, 2 * D])
    adaw_d = dram("ada_w", [2, D, 3 * D])
    w0_d = dram("w0", [8, D, 384])
    wo0_d = dram("wo0", [D, D])
    w1_d = dram("w1", [8, D, 1280])
    wo1_d = dram("wo1", [D, D])
    emask_d = dram("emask", [8, 128, 6 * 256])
    cst_d = dram("cst", [128, 256])
    y_d = dram("y", [S, D], kind="ExternalOutput")
    x_t = x_d.rearrange("(t p) d -> t p d", p=128)
    y_t = y_d.rearrange("(t p) d -> t p d", p=128)

    xs = nc.alloc_sbuf_tensor("xs", [128, NT, D], F32)
    hT = nc.alloc_sbuf_tensor("hT", [128, 8, S], BF16)
    cst = nc.alloc_sbuf_tensor("cstb", [128, 256], BF16)
    ident = cst[:, 0:128]
    bdones = cst[:, 128:256]
    ones_bf = nc.alloc_sbuf_tensor("ones_bf", [128, 128], BF16)
    ones_f = nc.alloc_sbuf_tensor("ones_f", [128, 128], F32)
    vecs = nc.alloc_sbuf_tensor("vecs_s", [128, NVEC], F32)
    cT = nc.alloc_sbuf_tensor("cTs", [128, 8], F32)
    sc2 = nc.alloc_sbuf_tensor("sc2", [128, 8, 2], F32)
    modT = nc.alloc_sbuf_tensor("modT", [128, 2, 16], F32)
    modA = nc.alloc_sbuf_tensor("modA", [128, 2, 8], F32)
    gate_rep = nc.alloc_sbuf_tensor("gate_rep", [128, 2, D], F32)
    ss = nc.alloc_sbuf_tensor("ss", [128, NT], F32)
    rs = nc.alloc_sbuf_tensor("rs", [128, NT], F32)
    gq8 = nc.alloc_sbuf_tensor("gq8", [128, 3], F32)
    epsb = nc.alloc_sbuf_tensor("epsb", [128, 1], F32)
    banks = [nc.alloc_psum_tensor(f"bank{i}", [128, 512], F32) for i in range(8)]
    bankb = [b.bitcast(BF16) for b in banks]
    BK = [f"bank{i}" for i in range(8)]

    arena = Arena(nc, (nc.sbuf_bytes_remaining - 64) // 32 * 32)
    TOP = (arena.nbytes - (6144 + 2048 + 3072)) // 32 * 32
    arena.reset(TOP)
    top_gsl0 = arena.alloc([8, 384], BF16)
    top_zsl = arena.alloc([8, 128], BF16)
    top_em0 = arena.alloc([6, 256], BF16)
    arena.peak = 0
    arena.limit = TOP
    l1_prefetched = []

    def l1_prefetch0():
        w1h = w1_d[0].rearrange("(c p) n -> p c n", p=128)
        P.dma("pool", "emask0", top_em0[:, :, :], emask_d[0].rearrange("p (a b) -> p a b", a=6), writes=["Em0"])
        P.dma("pool", "w1z", top_zsl[:, :, :], w1h[:, :, 1152:1280], writes=["w1z"])
        P.dma("pool", "w1s0", top_gsl0[:, :, :], w1h[:, :, 0:384], writes=["w1s0"])
        l1_prefetched.append(True)

    P.dma("pool", "cst", cst[:, :], cst_d, writes=["cst"])
    P.dma("sp", "vecsd", vecs[:, :], vecs_d, writes=["vecs"])
    P.dma("sp", "cTd", cT[:, :], cT_d, writes=["cT"])
    P.op("pool", lambda e: e.memset(ones_bf[:, :], 1.0), writes=["ones_bf"])
    P.op("pool", lambda e: e.memset(ones_f[:, :], 1.0), writes=["ones_f"])
    P.op("pool", lambda e: e.memset(epsb[:, :], EPS), writes=["epsb"])
    if first:
        for tt in range(NT):
            P.dma("sp", f"x{tt}", xs[:, tt, :], x_t[tt], writes=[f"x{tt}"])

    arena.reset()
    xn_setup = [arena.alloc([4, D], BF16) for _ in range(2)]
    screp = arena.alloc([8, 128], BF16)
    sc2h = arena.alloc([8, 2], BF16)
    adaslot = [arena.alloc([8, D], BF16) for _ in range(4)]
    rowb = arena.alloc([D], F32)
    P.op("act", lambda e: e.activation(out=sc2[:, :, 0], in_=cT[:, :], func=AF.Silu), reads=["cT"], writes=["sc2a"])
    P.op("act", lambda e: e.activation(out=sc2[:, :, 1], in_=cT[:, :], func=AF.Silu), reads=["cT"], writes=["sc2b"])
    P.op("dve", lambda e: e.tensor_copy(out=sc2h[:, :, :], in_=sc2[:, :, :]), reads=["sc2a", "sc2b"], writes=["sc2h"])
    for dc in range(8):
        P.op("dve", lambda e, dc=dc: e.tensor_scalar(out=screp[:, dc, :], in0=ones_f[:, :], scalar1=sc2[:, dc, 0:1],
                                                      scalar2=None, op0=ALU.mult),
             reads=["ones_f", "sc2a"], writes=[f"screp{dc}"])
    P.op("dve", lambda e: e.scalar_tensor_tensor(out=gq8[:, :], in0=vecs[:, V_QG:V_QG + 3], scalar=0.125,
                                                 in1=vecs[:, V_KG:V_KG + 3], op0=ALU.mult, op1=ALU.mult),
         reads=["vecs"], writes=["gq8"])
    piece_c = [0]
    ada_slots = {}

    def ada_dma(l, part):
        if (l, part) in ada_slots:
            return ada_slots[(l, part)]
        sl = piece_c[0] % 4
        piece_c[0] += 1
        P.dma("pool", f"ada{sl}", adaslot[sl][:, :, :],
              adaw_d[l].rearrange("(c p) n -> p c n", p=128)[:, :, part * D:(part + 1) * D], writes=[f"adaslot{sl}"])
        ada_slots[(l, part)] = sl
        return sl

    ada_dma(layers[0], 0)
    ada_dma(layers[0], 1)
    ada_dma(layers[0], 2)
    if len(layers) > 1:
        ada_dma(layers[1], 0)

    def emit_mod(l):
            adaw_l = adaw_d[l].rearrange("(c p) n -> p c n", p=128)
            for part in range(3):
                sl = ada_dma(l, part)
                if part == 2:
                    for half in range(2):
                        P.dma("sp", f"gbias{l}_{half}", gate_rep[:, l, half * 512:(half + 1) * 512],
                              gateb_d[0:1, l * D + half * 512:l * D + (half + 1) * 512].broadcast_to([128, 512]),
                              writes=[f"gate_rep{l}_{half}"])
                if part < 2:
                    for half in range(2):
                        def mmr(e, sl=sl, half=half):
                            r = None
                            for dc in range(8):
                                r = e.matmul(banks[1 + half][0:2, :], lhsT=sc2h[:, dc, :],
                                             rhs=adaslot[sl][:, dc, half * 512:(half + 1) * 512], start=(dc == 0), stop=(dc == 7))
                            return r
                        P.op("pe", mmr, reads=[f"adaslot{sl}", "sc2h"], writes=[BK[1 + half]])
                        P.op("act", lambda e, half=half: e.activation(out=rowb[0:1, half * 512:(half + 1) * 512],
                                                                      in_=banks[1 + half][0:1, :], func=AF.Copy),
                             reads=[BK[1 + half]], writes=[f"rowb{half}"])

                    def mm(e):
                        r = None
                        for j in range(8):
                            r = e.matmul(banks[0][:, 2 * j:2 * j + 2], lhsT=rowb[0:1, j * 128:(j + 1) * 128],
                                         rhs=ones_f[0:1, 0:2], start=True, stop=True)
                        return r
                    P.op("pe", mm, reads=["rowb0", "rowb1", "ones_f"], writes=[BK[0]])
                    P.op("dve", lambda e, l=l, part=part: e.tensor_tensor(
                        out=modT[:, l, part * 8:(part + 1) * 8], in0=banks[0][:, 0:16:2],
                        in1=vecs[:, (V_SHB, V_SCB)[part][l]:(V_SHB, V_SCB)[part][l] + 8], op=ALU.add),
                        reads=[BK[0], "vecs"], writes=[f"modT{l}_{part}"])
                else:
                    for half in range(2):
                        def mm(e, sl=sl, half=half, l=l):
                            r = None
                            for dc in range(8):
                                r = e.matmul(banks[1 + half][:, :], lhsT=screp[:, dc, :],
                                             rhs=adaslot[sl][:, dc, half * 512:(half + 1) * 512], start=(dc == 0), stop=(dc == 7))
                            return r
                        P.op("pe", mm, reads=[f"adaslot{sl}"] + [f"screp{dc}" for dc in range(8)], writes=[BK[1 + half]])
                        P.op("dve", lambda e, l=l, half=half: e.tensor_tensor(
                            out=gate_rep[:, l, half * 512:(half + 1) * 512], in0=banks[1 + half][:, :],
                            in1=gate_rep[:, l, half * 512:(half + 1) * 512], op=ALU.add),
                            reads=[BK[1 + half], f"gate_rep{l}_{half}"], writes=[f"gate_rep{l}_{half}"])
            P.op("dve", lambda e, l=l: e.scalar_tensor_tensor(
                out=modA[:, l, :], in0=modT[:, l, 8:16], scalar=1.0, in1=vecs[:, V_NG[l]:V_NG[l] + 8],
                op0=ALU.add, op1=ALU.mult), reads=[f"modT{l}_1", "vecs"], writes=[f"modA{l}"])

    EARLY_NORM = (layers[0] == 0)
    stop_at = STOP
    norm_done = set()

    def maybe_stop(tag):
        if stop_at == tag:
            for tt in range(NT):
                P.dma("sp", f"x{tt}", y_t[tt], xs[:, tt, :], reads=[f"x{tt}"])
            raise _Stop()

    def norm_E(l, tg, xnb, xk, extra_w=(), xn_engs=("pool", "dve")):
        for j in range(4):
            tt = tg * 4 + j
            P.op("act", lambda e, tt=tt, j=j: e.activation(out=xnb[:, j, :], in_=xs[:, tt, :], func=AF.Square,
                                                           accum_out=ss[:, tt:tt + 1]),
                 reads=[f"x{tt}"], writes=[f"{xk}_{j}", f"ss{tt}"] + list(extra_w))
        sreg = [f"ss{tg * 4 + j}" for j in range(4)]
        P.op("act", lambda e: e.activation(out=ss[:, tg * 4:tg * 4 + 4], in_=ss[:, tg * 4:tg * 4 + 4], func=AF.Sqrt,
                                           scale=1.0 / D, bias=epsb[:, 0:1]), reads=sreg + ["epsb"], writes=sreg)
        P.op("dve", lambda e: e.reciprocal(out=rs[:, tg * 4:tg * 4 + 4], in_=ss[:, tg * 4:tg * 4 + 4]),
             reads=sreg, writes=[f"rs{tg}"])
        for j in range(4):
            tt = tg * 4 + j
            eng = xn_engs[j % 2]
            P.op(eng, lambda e, tt=tt, j=j: e.tensor_scalar(
                out=xnb[:, j, :], in0=xs[:, tt, :], scalar1=rs[:, tt:tt + 1], scalar2=1.0,
                op0=ALU.mult, op1=ALU.mult), reads=[f"x{tt}", f"rs{tg}"], writes=[f"{xk}_{j}"])

    def norm_T(l, tg, xnb, xk, all_act=False):
        for c in range(8):
            pb = 6 + (c % 2)

            def tr(e, c=c, pb=pb):
                r = None
                for j in range(4):
                    r = e.transpose(out=bankb[pb][:, j * 128:(j + 1) * 128], in_=xnb[:, j, c * 128:(c + 1) * 128],
                                    identity=ident)
                return r
            P.op("pe", tr, reads=[f"{xk}_{j}" for j in range(4)] + ["cst"], writes=[BK[pb]])
            if c % 2 == 0 or all_act:
                P.op("act", lambda e, c=c, pb=pb: e.activation(
                    out=hT[:, c, tg * 512:(tg + 1) * 512], in_=bankb[pb][:, 0:512], func=AF.Identity,
                    scale=modA[:, l, c:c + 1], bias=modT[:, l, c:c + 1]),
                    reads=[BK[pb], f"modA{l}", f"modT{l}_0"], writes=[f"hT{c}_{tg}"])
            else:
                P.op("dve", lambda e, c=c, pb=pb: e.tensor_scalar(
                    out=hT[:, c, tg * 512:(tg + 1) * 512], in0=bankb[pb][:, 0:512],
                    scalar1=modA[:, l, c:c + 1], scalar2=modT[:, l, c:c + 1], op0=ALU.mult, op1=ALU.add),
                    reads=[BK[pb], f"modA{l}", f"modT{l}_0"], writes=[f"hT{c}_{tg}"])

    def phase_norm(l, xn=None, barrier=True, mod_first=None):
        if xn is None:
            arena.reset()
            xn = [arena.alloc([4, D], BF16) for _ in range(2)]
        norm_E(l, 0, xn[0], "xn0")
        if mod_first is not None:
            norm_E(l, 1, xn[1], "xn1")
            emit_mod(mod_first)
        for tg in range(4):
            if tg + 1 < 4 and not (mod_first is not None and tg == 0):
                norm_E(l, tg + 1, xn[(tg + 1) % 2], f"xn{(tg + 1) % 2}")
            norm_T(l, tg, xn[tg % 2], f"xn{tg % 2}")
        if barrier:
            P.barrier()

    def hT_regs(tiles):
        return [f"hT{dc}_{tg}" for dc in range(8) for tg in tiles]

    def phase_out(l, wo_d, uT, ureg, is_last):
        stg = [arena.alloc([D], F32) for _ in range(8)]
        wog = arena.alloc([8, D], BF16)
        for kc in range(8):
            P.dma("sp", f"wostgL{l}_{kc}", stg[kc][:, :], wo_d[kc * 128:(kc + 1) * 128, :], writes=[f"wostgL{kc}"])
        for kc in range(8):
            P.op("dve" if kc % 2 == 0 else "pool", lambda e, kc=kc: e.tensor_tensor(
                out=wog[:, kc, :], in0=stg[kc][:, :], in1=gate_rep[:, l, :], op=ALU.mult),
                reads=[f"wostgL{kc}", f"gate_rep{l}_0", f"gate_rep{l}_1"], writes=[f"wog{kc}"])
        maybe_stop("O1")
        n = 0
        for tt in range(NT):
            for half in range(2):
                pb = 4 + n % 4
                n += 1

                def mm(e, tt=tt, half=half, pb=pb):
                    r = None
                    for kc in range(8):
                        r = e.matmul(banks[pb][:, :], lhsT=uT[:, kc, tt * 128:(tt + 1) * 128],
                                     rhs=wog[:, kc, half * 512:(half + 1) * 512], start=(kc == 0), stop=(kc == 7))
                    return r
                P.op("pe", mm, reads=[f"wog{kc}" for kc in range(8)] + ureg(tt), writes=[BK[pb]])
                P.op("dve", lambda e, tt=tt, half=half, pb=pb: e.tensor_tensor(
                    out=xs[:, tt, half * 512:(half + 1) * 512], in0=banks[pb][:, :],
                    in1=xs[:, tt, half * 512:(half + 1) * 512], op=ALU.add),
                    reads=[BK[pb], f"x{tt}"], writes=[f"x{tt}"])
            if is_last and stop_at != "O2":
                P.dma("sp", f"x{tt}", y_t[tt], xs[:, tt, :], reads=[f"x{tt}"])
        maybe_stop("O2")
        P.barrier()

    def layer0():
        l = 0
        phase_norm(l, xn_setup, barrier=False, mod_first=0)
        for l2 in layers[1:]:
            emit_mod(l2)
        P.barrier()
        maybe_stop("norm")
        arena.reset()
        cTb = arena.alloc([8, S], BF16)
        S1 = arena.alloc([S], F32)
        S2 = arena.alloc([S], F32)
        zsl = [arena.alloc([8, 128], BF16) for _ in range(8)]
        mark = arena.off
        wsl = [arena.alloc([8, 256], BF16) for _ in range(2)]
        diag = [arena.alloc([KW, 128], BF16) for _ in range(2)]
        ub = [arena.alloc([S + 32], BF16) for _ in range(2)]
        sig = [arena.alloc([512], F32) for _ in range(2)]
        csq = [arena.alloc([512], BF16) for _ in range(2)]
        for i in range(2):
            P.op("pool", lambda e, i=i: e.memset(ub[i][:, 0:32], 0.0), writes=[f"upad{i}"])

        def load_w(cc):
            sl = cc % 2
            w0c = w0_d[cc].rearrange("(c p) n -> p c n", p=128)
            P.dma("pool", f"w0s{sl}", wsl[sl][:, :, :], w0c[:, :, 0:256], writes=[f"w0s{sl}"])

        def make_diag(cc):
            sl = cc % 2
            for k in range(KW):
                P.op("pool", lambda e, k=k, sl=sl, cc=cc: e.tensor_scalar(
                    out=diag[sl][:, k, :], in0=ident, scalar1=vecs[:, V_CONVW + cc * KW + k:V_CONVW + cc * KW + k + 1],
                    scalar2=1.0, op0=ALU.mult, op1=ALU.mult), reads=["cst", "vecs"], writes=[f"diag{sl}_{k}"])

        def stats(cc, t4):
            P.op("pe", lambda e, cc=cc, t4=t4: e.matmul(banks[6][:, :], lhsT=ones_bf[:, :],
                                                        rhs=cTb[:, cc, t4 * 512:(t4 + 1) * 512], start=True, stop=True),
                 reads=[f"c{cc}_{t4}", "ones_bf"], writes=[BK[6]])
            P.op("pe", lambda e, t4=t4: e.matmul(banks[7][:, :], lhsT=ones_bf[:, :], rhs=csq[t4 % 2][:, :],
                                                 start=True, stop=True),
                 reads=[f"csq{t4 % 2}", "ones_bf"], writes=[BK[7]])
            for (Sx, pb, nm) in ((S1, 6, "S1"), (S2, 7, "S2")):
                if cc == 0:
                    P.op("dve", lambda e, Sx=Sx, pb=pb, t4=t4: e.tensor_copy(out=Sx[:, t4 * 512:(t4 + 1) * 512],
                                                                             in_=banks[pb][:, :]),
                         reads=[BK[pb]], writes=[f"{nm}_{t4}"])
                else:
                    P.op("dve", lambda e, Sx=Sx, pb=pb, t4=t4: e.tensor_tensor(
                        out=Sx[:, t4 * 512:(t4 + 1) * 512], in0=banks[pb][:, :], in1=Sx[:, t4 * 512:(t4 + 1) * 512],
                        op=ALU.add), reads=[BK[pb], f"{nm}_{t4}"], writes=[f"{nm}_{t4}"])

        load_w(0)
        load_w(1)
        make_diag(0)
        pend = None
        for cc in range(8):
            sl = cc % 2
            if cc + 1 < 8:
                make_diag(cc + 1)
            for t4 in range(4):
                pa, pg = 0 + (t4 % 2), 2 + (t4 % 2)

                def mm(e, sl=sl, t4=t4, pa=pa, pg=pg):
                    r = None
                    for dc in range(8):
                        r = e.matmul(banks[pa][:, :], lhsT=wsl[sl][:, dc, 0:128], rhs=hT[:, dc, t4 * 512:(t4 + 1) * 512],
                                     start=(dc == 0), stop=(dc == 7))
                    for dc in range(8):
                        r = e.matmul(banks[pg][:, :], lhsT=wsl[sl][:, dc, 128:256], rhs=hT[:, dc, t4 * 512:(t4 + 1) * 512],
                                     start=(dc == 0), stop=(dc == 7))
                    return r
                P.op("pe", mm, reads=[f"w0s{sl}"] + hT_regs([t4]), writes=[BK[pa], BK[pg]])
                if pend is not None and t4 == 0:
                    stats(*pend)
                    pend = None
                P.op("act", lambda e, t4=t4, pg=pg: e.activation(out=sig[t4 % 2][:, :], in_=banks[pg][:, :], func=AF.Sigmoid),
                     reads=[BK[pg]], writes=[f"sig{t4 % 2}"])
                P.op("dve", lambda e, t4=t4, pa=pa, sl=sl: e.tensor_tensor(
                    out=ub[sl][:, 32 + t4 * 512:32 + (t4 + 1) * 512], in0=banks[pa][:, :], in1=sig[t4 % 2][:, :], op=ALU.mult),
                    reads=[BK[pa], f"sig{t4 % 2}"], writes=[f"u{sl}_{t4}"])
            if cc + 2 < 8:
                load_w(cc + 2)
            P.dma("pool", f"w0z{cc}", zsl[cc][:, :, :], w0_d[cc].rearrange("(c p) n -> p c n", p=128)[:, :, 256:384],
                  writes=[f"w0z{cc}"])
            for t4 in range(4):
                pc = 4 + (t4 % 2)

                def mmc(e, sl=sl, t4=t4, pc=pc):
                    r = None
                    for k in range(KW):
                        o = 2 + t4 * 512 + k
                        r = e.matmul(banks[pc][:, :], lhsT=diag[sl][:, k, :], rhs=ub[sl][:, o:o + 512],
                                     start=(k == 0), stop=(k == KW - 1))
                    return r
                ur = [f"u{sl}_{t4}"] + ([f"u{sl}_{t4 - 1}"] if t4 > 0 else [f"upad{sl}"])
                P.op("pe", mmc, reads=ur + [f"diag{sl}_{k}" for k in range(KW)], writes=[BK[pc]])
                if pend is not None:
                    stats(*pend)
                P.op("act", lambda e, cc=cc, t4=t4, pc=pc: e.activation(
                    out=cTb[:, cc, t4 * 512:(t4 + 1) * 512], in_=banks[pc][:, :], func=AF.Identity,
                    bias=vecs[:, V_CONVB + cc:V_CONVB + cc + 1], scale=1.0),
                    reads=[BK[pc], "vecs"], writes=[f"c{cc}_{t4}"])
                P.op("act", lambda e, cc=cc, t4=t4, pc=pc: e.activation(
                    out=csq[t4 % 2][:, :], in_=banks[pc][:, :], func=AF.Square,
                    bias=vecs[:, V_CONVB + cc:V_CONVB + cc + 1], scale=1.0),
                    reads=[BK[pc], "vecs"], writes=[f"csq{t4 % 2}"])
                pend = (cc, t4)
        stats(*pend)
        P.barrier()
        maybe_stop("A")
        arena.reset(mark)
        t1 = [arena.alloc([512], F32) for _ in range(2)]
        t2 = [arena.alloc([512], F32) for _ in range(2)]
        szt = [arena.alloc([512], BF16) for _ in range(2)]
        stg_all = arena.alloc([2, D], F32)
        stg = [stg_all[:, 0, :], stg_all[:, 1, :]]
        xn1 = stg_all.bitcast(BF16).rearrange("p a (b c) -> p (a b) c", c=D)
        wog = arena.alloc([8, D], BF16)
        def wog_dma(kc):
            sl = kc % 2
            P.dma("sp", f"wostg{sl}", stg[sl], wo0_d[kc * 128:(kc + 1) * 128, :], writes=[f"wostg{sl}"])

        def wog_mul(kc):
            sl = kc % 2
            P.op("pool", lambda e, kc=kc, sl=sl: e.tensor_tensor(out=wog[:, kc, :], in0=stg[sl],
                                                                 in1=gate_rep[:, 0, :], op=ALU.mult),
                 reads=[f"wostg{sl}", "gate_rep0_0", "gate_rep0_1"], writes=[f"wog{kc}"])

        def finalize(t4):
            sl_ = slice(t4 * 512, (t4 + 1) * 512)
            P.op("dve", lambda e, sl_=sl_: e.tensor_scalar(out=S1[:, sl_], in0=S1[:, sl_], scalar1=1.0 / D, scalar2=None,
                                                           op0=ALU.mult), reads=[f"S1_{t4}"], writes=[f"S1_{t4}"])
            P.op("dve", lambda e, sl_=sl_: e.tensor_tensor(out=t1[0][:, :], in0=S1[:, sl_], in1=S1[:, sl_], op=ALU.mult),
                 reads=[f"S1_{t4}"], writes=["t1_0"])
            P.op("dve", lambda e, sl_=sl_: e.scalar_tensor_tensor(out=S2[:, sl_], in0=S2[:, sl_], scalar=1.0 / D,
                                                                  in1=t1[0][:, :], op0=ALU.mult, op1=ALU.subtract),
                 reads=[f"S2_{t4}", "t1_0"], writes=[f"S2_{t4}"])
            P.op("act", lambda e, sl_=sl_: e.activation(out=S2[:, sl_], in_=S2[:, sl_], func=AF.Ln, bias=epsb[:, 0:1],
                                                        scale=1.0), reads=[f"S2_{t4}", "epsb"], writes=[f"S2_{t4}"])
            P.op("act", lambda e, sl_=sl_: e.activation(out=S2[:, sl_], in_=S2[:, sl_], func=AF.Exp, scale=-0.5),
                 reads=[f"S2_{t4}"], writes=[f"S2_{t4}"])
            P.op("dve", lambda e, sl_=sl_: e.scalar_tensor_tensor(out=S1[:, sl_], in0=S1[:, sl_], scalar=-1.0,
                                                                  in1=S2[:, sl_], op0=ALU.mult, op1=ALU.mult),
                 reads=[f"S1_{t4}", f"S2_{t4}"], writes=[f"S1_{t4}"])
        cnt0 = {"n": 0, "o": 0}

        def z_tiles(t4, wog0=None):
            sl_ = slice(t4 * 512, (t4 + 1) * 512)
            pend_uu = None
            with_wog = wog0 is not None
            if with_wog:
                wog_dma(wog0)
                wog_dma(wog0 + 1)
            for cc in range(8):
                b = cnt0["n"] % 2
                pz = (cnt0["n"] % 8) if t4 == 0 else (cnt0["n"] % 4)
                cnt0["n"] += 1

                def mm(e, cc=cc, pz=pz):
                    r = None
                    for dc in range(8):
                        r = e.matmul(banks[pz][:, :], lhsT=zsl[cc][:, dc, :], rhs=hT[:, dc, t4 * 512:(t4 + 1) * 512],
                                     start=(dc == 0), stop=(dc == 7))
                    return r
                P.op("pe", mm, reads=[f"w0z{cc}"] + hT_regs([t4]), writes=[BK[pz]])
                P.op("act", lambda e, b=b, pz=pz: e.activation(out=szt[b][:, :], in_=banks[pz][:, :], func=AF.Silu),
                     reads=[BK[pz]], writes=[f"szt{b}"])
                P.op("dve", lambda e, b=b, cc=cc: e.tensor_tensor(out=t1[b][:, :], in0=cTb[:, cc, sl_],
                                                                   in1=S2[:, sl_], op=ALU.mult),
                     reads=[f"c{cc}_{t4}", f"S2_{t4}"], writes=[f"t1_{b}"])
                P.op("dve", lambda e, b=b: e.tensor_tensor(out=t2[b][:, :], in0=t1[b][:, :], in1=S1[:, sl_], op=ALU.add),
                     reads=[f"t1_{b}", f"S1_{t4}"], writes=[f"t2_{b}"])
                P.op("act", lambda e, b=b, cc=cc: e.activation(out=t2[b][:, :], in_=t2[b][:, :], func=AF.Silu,
                                                               scale=vecs[:, V_LNG + cc:V_LNG + cc + 1],
                                                               bias=vecs[:, V_LNB + cc:V_LNB + cc + 1]),
                     reads=[f"t2_{b}", "vecs"], writes=[f"t2_{b}"])
                if pend_uu is not None:
                    pend_uu()

                def uu(b=b, cc=cc):
                    P.op("pool", lambda e: e.tensor_tensor(out=cTb[:, cc, sl_], in0=t2[b][:, :], in1=szt[b][:, :], op=ALU.mult),
                         reads=[f"t2_{b}", f"szt{b}"], writes=[f"c{cc}_{t4}"])
                pend_uu = uu
                if with_wog and cc % 2 == 1:
                    kc_ = wog0 + cc // 2
                    wog_mul(kc_)
                    if cc // 2 + 2 < 4:
                        wog_dma(kc_ + 2)
            pend_uu()

        def out_tiles(t4):
            for tt in range(4 * t4, 4 * t4 + 4):
                for half in range(2):
                    pb = 4 + cnt0["o"] % 4
                    cnt0["o"] += 1

                    def mm(e, tt=tt, half=half, pb=pb):
                        r = None
                        for kc in range(8):
                            r = e.matmul(banks[pb][:, :], lhsT=cTb[:, kc, tt * 128:(tt + 1) * 128],
                                         rhs=wog[:, kc, half * 512:(half + 1) * 512], start=(kc == 0), stop=(kc == 7))
                        return r
                    P.op("pe", mm, reads=[f"wog{kc}" for kc in range(8)] + [f"c{kc}_{t4}" for kc in range(8)],
                         writes=[BK[pb]])
                    P.op("dve", lambda e, tt=tt, half=half, pb=pb: e.tensor_tensor(
                        out=xs[:, tt, half * 512:(half + 1) * 512], in0=banks[pb][:, :],
                        in1=xs[:, tt, half * 512:(half + 1) * 512], op=ALU.add),
                        reads=[BK[pb], f"x{tt}"], writes=[f"x{tt}"])
                if last and layers[-1] == 0:
                    P.dma("sp", f"x{tt}", y_t[tt], xs[:, tt, :], reads=[f"x{tt}"])

        fuse_next = (1 in layers)
        if fuse_next:
            l1_prefetch0()
        finalize(0)
        z_tiles(0)
        finalize(1)
        z_tiles(1, wog0=0)
        finalize(2)
        z_tiles(2, wog0=4)
        for t4 in range(4):
            out_tiles(t4)
            if t4 == 0:
                finalize(3)
                z_tiles(3)
            if fuse_next:
                if t4 > 0:
                    norm_T(1, t4 - 1, xn1, "xn1", all_act=True)
                norm_E(1, t4, xn1, "xn1", extra_w=("wostg0", "wostg1"), xn_engs=("pool", "pool"))
        if fuse_next:
            norm_T(1, 3, xn1, "xn1", all_act=True)
            norm_done.add(1)
        maybe_stop("B")
        P.barrier()

    def layer1():
        l = 1
        if 1 not in norm_done:
            phase_norm(l)
        maybe_stop("n1")
        arena.reset()
        uT = arena.alloc([8, S], BF16)
        mark = arena.off
        qk = [[arena.alloc([S], BF16) for _ in range(2)] for _ in range(3)]
        Vt = [arena.alloc([16, 192], BF16) for _ in range(3)]
        Em = [top_em0, arena.alloc([6, 256], BF16)]
        gsl = [top_gsl0, arena.alloc([8, 384], BF16)]
        zsl = top_zsl
        sz = arena.alloc([S], BF16)
        sqt = [arena.alloc([512], BF16) for _ in range(2)]
        sdt = [arena.alloc([512], F32) for _ in range(2)]
        NPT = 4
        PTall = arena.alloc([NPT * 512], BF16)
        PT = [PTall[:, i * 512:(i + 1) * 512] for i in range(NPT)]
        rd = arena.alloc([512], F32)
        ot = arena.alloc([512], F32)
        sq4 = [(sqt[0], "sqt0"), (sqt[1], "sqt1"), (rd.bitcast(BF16)[:, 0:512], "rdA"), (ot.bitcast(BF16)[:, 0:512], "otA")]
        for g in range(3):
            P.op("pool", lambda e, g=g: e.memset(Vt[g][:, :, 64:128], 1.0), writes=[f"Vones{g}"])

        def tok_tile(g, dc, t4):
            if g == 0:
                return hT[:, dc, t4 * 512:(t4 + 1) * 512]
            if g == 1:
                return hT[:, dc, t4::4]
            return hT[:, dc, :].rearrange("p (i r) -> p r i", r=16)[:, 4 * t4:4 * t4 + 4, :]

        def tok_block(g, dc, blk):
            if g == 0:
                return hT[:, dc, blk * 128:(blk + 1) * 128]
            if g == 1:
                r, b = blk // 4, blk % 4
                s0 = 4 * 128 * b + r
                return hT[:, dc, s0:s0 + 509:4]
            return hT[:, dc, blk::16]

        GSLOT = (0, 1, 0)
        PPB = (6, 7, 0, 1, 2, 3)
        all_h = hT_regs(range(4))

        def load_g(hp, g):
            w1h = w1_d[hp].rearrange("(c p) n -> p c n", p=128)
            sl = GSLOT[g]
            P.dma("pool", f"w1s{sl}", gsl[sl][:, :, :], w1h[:, :, g * 384:(g + 1) * 384], writes=[f"w1s{sl}"])

        def prefetch(hp):
            w1h = w1_d[hp].rearrange("(c p) n -> p c n", p=128)
            if hp == 0 and l1_prefetched:
                load_g(hp, 1)
                return
            P.dma("pool", f"emask{hp % 2}", Em[hp % 2][:, :, :], emask_d[hp].rearrange("p (a b) -> p a b", a=6),
                  writes=[f"Em{hp % 2}"])
            P.dma("pool", "w1z", zsl[:, :, :], w1h[:, :, 1152:1280], writes=["w1z"])
            load_g(hp, 0)
            load_g(hp, 1)

        cnt = {"p": 0, "n": 0, "s": 0, "t": 0, "q": 0}

        def proj_items(hp):
            items = []
            for g in range(3):
                sl = GSLOT[g]
                for j in range(2):
                    for t4 in range(4):
                        def front(g=g, sl=sl, j=j, t4=t4):
                            pp = PPB[cnt["p"] % 6]
                            cnt["p"] += 1
                            hr = hT_regs([t4])

                            def mm(e):
                                r = None
                                for dc in range(8):
                                    r = e.matmul(banks[pp][:, :], lhsT=gsl[sl][:, dc, j * 128:(j + 1) * 128],
                                                 rhs=hT[:, dc, t4 * 512:(t4 + 1) * 512], start=(dc == 0), stop=(dc == 7))
                                return r
                            P.op("pe", mm, reads=[f"w1s{sl}"] + hr, writes=[BK[pp]])
                            sqx, sqn = sq4[cnt["q"] % 4]
                            cnt["q"] += 1
                            P.op("act", lambda e: e.activation(out=sqx[:, :], in_=banks[pp][:, :], func=AF.Square),
                                 reads=[BK[pp]], writes=[sqn])
                            return (pp, sqx, sqn)

                        def back(tok, g=g, j=j, t4=t4):
                            pp, sqx, sqn = tok
                            b = cnt["n"] % 2
                            cnt["n"] += 1
                            pn = 4 + b
                            P.op("pe", lambda e: e.matmul(banks[pn][:, :], lhsT=bdones, rhs=sqx[:, :], start=True, stop=True),
                                 reads=[sqn, "cst"], writes=[BK[pn]])
                            P.op("act", lambda e: e.activation(out=sdt[b][:, :], in_=banks[pn][:, :], func=AF.Ln,
                                                               scale=1.0 / 64, bias=epsb[:, 0:1]),
                                 reads=[BK[pn], "epsb"], writes=[f"sdt{b}"])
                            P.op("act", lambda e: e.activation(out=sdt[b][:, :], in_=sdt[b][:, :], func=AF.Exp, scale=-0.5),
                                 reads=[f"sdt{b}"], writes=[f"sdt{b}"])
                            gv = gq8[:, g:g + 1] if j == 0 else vecs[:, V_KG + g:V_KG + g + 1]
                            if g == 0:
                                o_ap, i0_ap, i1_ap = qk[g][j][:, t4 * 512:(t4 + 1) * 512], banks[pp][:, :], sdt[b][:, :]
                                wr = [f"qk{g}_{j}_{t4}"]
                            else:
                                R_ = DILS[g]
                                n_ = 512 // R_
                                o_ap = qk[g][j][:, :].rearrange("p (r i) -> p r i", r=R_)[:, :, t4 * n_:(t4 + 1) * n_]
                                i0_ap = banks[pp][:, :].rearrange("p (i r) -> p r i", r=R_)
                                i1_ap = sdt[b][:, :].rearrange("p (i r) -> p r i", r=R_)
                                wr = [f"qk{g}_{j}_{q_}" for q_ in range(4)]
                            if j == 0:
                                P.op("dve", lambda e: e.scalar_tensor_tensor(
                                    out=o_ap, in0=i0_ap, scalar=gv, in1=i1_ap, op0=ALU.mult, op1=ALU.mult),
                                    reads=[BK[pp], f"sdt{b}", "gq8", "vecs"], writes=wr)
                            else:
                                P.op("dve", lambda e: e.tensor_tensor(out=o_ap, in0=i0_ap, in1=i1_ap, op=ALU.mult),
                                     reads=[BK[pp], f"sdt{b}"], writes=wr)
                        items.append((front, back))
                for t4 in range(4):
                    def frontvt(g=g, sl=sl, t4=t4):
                        pp = PPB[cnt["p"] % 6]
                        cnt["p"] += 1

                        def mmv(e):
                            r = None
                            for dc in range(8):
                                r = e.matmul(banks[pp][:, :], lhsT=gsl[sl][:, dc, 256:384],
                                             rhs=hT[:, dc, t4 * 512:(t4 + 1) * 512], start=(dc == 0), stop=(dc == 7))
                            return r
                        P.op("pe", mmv, reads=[f"w1s{sl}"] + hT_regs([t4]), writes=[BK[pp]])
                        if g == 0 and t4 == 3:
                            load_g(hp, 2)
                        return pp

                    def backvt(pp, g=g, t4=t4):
                        if g == 0:
                            o_ap, i_ap, wr = PTall[:, t4 * 512:(t4 + 1) * 512], banks[pp][:, :], [f"PT{t4}"]
                        else:
                            R_ = DILS[g]
                            n_ = 512 // R_
                            o_ap = PTall[:, 0:2048].rearrange("p (r i) -> p r i", r=R_)[:, :, t4 * n_:(t4 + 1) * n_]
                            i_ap = banks[pp][:, :].rearrange("p (i r) -> p r i", r=R_)
                            wr = [f"PT{q_}" for q_ in range(4)]
                        P.op("dve", lambda e: e.tensor_copy(out=o_ap, in_=i_ap), reads=[BK[pp]], writes=wr)
                    items.append((frontvt, backvt))
                items.append(("flush", None))
                for b4 in range(4):
                    def frontv(g=g, b4=b4):
                        pv = PPB[cnt["p"] % 6]
                        cnt["p"] += 1

                        def trv(e):
                            r = None
                            for i in range(4):
                                blk = b4 * 4 + i
                                r = e.transpose(out=bankb[pv][:, i * 128:(i + 1) * 128], in_=PTall[:, blk * 128:(blk + 1) * 128],
                                                identity=ident)
                            return r
                        P.op("pe", trv, reads=[f"PT{b4}", "cst"], writes=[BK[pv]])
                        return pv

                    def backv(pv, g=g, b4=b4):
                        src = bankb[pv][:, 0:512].rearrange("p (b c) -> p b c", c=128)
                        P.op("dve", lambda e: e.tensor_copy(out=Vt[g][:, b4 * 4:b4 * 4 + 4, 0:64], in_=src[:, :, 0:64]),
                             reads=[BK[pv]], writes=[f"V{g}_{b4}a"])
                        P.op("dve", lambda e: e.tensor_copy(out=Vt[g][:, b4 * 4:b4 * 4 + 4, 128:192], in_=src[:, :, 64:128]),
                             reads=[BK[pv]], writes=[f"V{g}_{b4}b"])
                    items.append((frontv, backv))
            for t4 in range(4):
                def frontz(t4=t4):
                    pp = PPB[cnt["p"] % 6]
                    cnt["p"] += 1

                    def mmz(e):
                        r = None
                        for dc in range(8):
                            r = e.matmul(banks[pp][:, :], lhsT=zsl[:, dc, :], rhs=hT[:, dc, t4 * 512:(t4 + 1) * 512],
                                         start=(dc == 0), stop=(dc == 7))
                        return r
                    P.op("pe", mmz, reads=["w1z"] + hT_regs([t4]), writes=[BK[pp]])
                    return pp

                def backz(pp, t4=t4):
                    P.op("act", lambda e: e.activation(out=sz[:, t4 * 512:(t4 + 1) * 512], in_=banks[pp][:, :], func=AF.Silu),
                         reads=[BK[pp]], writes=[f"sz{t4}"])
                items.append((frontz, backz))
            return items

        def run_items(items, look=1, group=1):
            if group > 1:
                return run_items_grouped(items, group)
            pend = []
            deferred = []
            for (front, back) in items:
                if front == "flush":
                    for bk_, tk_ in pend:
                        bk_(tk_)
                    pend = []
                    continue
                if front == "evac":
                    for bk_, tk_ in pend:
                        bk_(tk_)
                    pend = []
                    deferred.append(back)
                    continue

                tok = front()
                pend.append((back, tok))
                if deferred and len(pend) >= look + 1:
                    for d_ in deferred:
                        d_()
                    deferred.clear()
                if len(pend) > look:
                    bk_, tk_ = pend.pop(0)
                    bk_(tk_)
            for bk_, tk_ in pend:
                bk_(tk_)
            for d_ in deferred:
                d_()

        def run_items_grouped(items, group):
            prev = []
            cur = []
            deferred = []

            def flush_prev():
                for bk_, tk_ in prev:
                    bk_(tk_)
                prev.clear()

            for (front, back) in items:
                if front == "evac":
                    flush_prev()
                    for bk_, tk_ in cur:
                        bk_(tk_)
                    cur.clear()
                    deferred.append(back)
                    continue
                tok = front()
                cur.append((back, tok))
                if len(cur) == group:
                    if deferred:
                        for d_ in deferred:
                            d_()
                        deferred.clear()
                    flush_prev()
                    prev.extend(cur)
                    cur.clear()
            flush_prev()
            for bk_, tk_ in cur:
                bk_(tk_)
            for d_ in deferred:
                d_()

        def attn_items(hp, h):
            p0 = 64 * h
            Emh = Em[hp % 2]
            emr = f"Em{hp % 2}"
            started = [False] * 4
            items = []
            for g in range(3):
                if g == 0:
                    jobs = [(kb, 256 if kb < 15 else 128) for kb in range(16)]
                elif g == 1:
                    jobs = [(kb, 256 if kb % 4 < 3 else 128) for kb in range(16)]
                else:
                    jobs = [(kb, 128) for kb in range(16)]
                packs = []
                cur, used = [], 0
                for kb, ncol in jobs:
                    if used + ncol > 512:
                        packs.append(cur)
                        cur, used = [], 0
                    cur.append((kb, ncol, used))
                    used += ncol
                if cur:
                    packs.append(cur)
                for pack in packs:
                    def front(pack=pack, g=g):
                        ps = 4 + cnt["s"] % 4
                        cnt["s"] += 1
                        pt = cnt["t"] % NPT
                        cnt["t"] += 1
                        tot = pack[-1][2] + pack[-1][1]
                        qregs = set()
                        for kb, ncol, off in pack:
                            qregs.add(f"qk{g}_1_{kb // 4}")
                            qregs.add(f"qk{g}_0_{kb // 4}")
                            qregs.add(f"qk{g}_0_{min(15, kb + 1) // 4}")

                        def mms(e):
                            r = None
                            for kb, ncol, off in pack:
                                r = e.matmul(banks[ps][:, off:off + ncol], lhsT=qk[g][1][p0:p0 + 64, kb * 128:(kb + 1) * 128],
                                             rhs=qk[g][0][p0:p0 + 64, kb * 128:kb * 128 + ncol], start=True, stop=True)
                            return r
                        P.op("pe", mms, reads=sorted(qregs), writes=[BK[ps]])
                        P.op("act", lambda e: e.activation(out=PT[pt][:, 0:tot], in_=banks[ps][:, 0:tot], func=AF.Exp),
                             reads=[BK[ps]], writes=[f"PT{pt}"])
                        runs = []
                        for kb, ncol, off in pack:
                            if runs and runs[-1][1] == ncol and runs[-1][0] + runs[-1][1] * runs[-1][2] == off:
                                runs[-1][2] += 1
                            else:
                                runs.append([off, ncol, 1])
                        for off, ncol, cntk in runs:
                            if cntk == 1:
                                P.op("dve", lambda e, off=off, ncol=ncol: e.tensor_tensor(
                                    out=PT[pt][:, off:off + ncol], in0=PT[pt][:, off:off + ncol],
                                    in1=Emh[:, h * 3 + g, 0:ncol], op=ALU.mult), reads=[f"PT{pt}", emr], writes=[f"PT{pt}"])
                            else:
                                P.op("dve", lambda e, off=off, ncol=ncol, cntk=cntk: e.tensor_tensor(
                                    out=PT[pt][:, off:off + ncol * cntk].rearrange("p (a b) -> p a b", a=cntk),
                                    in0=PT[pt][:, off:off + ncol * cntk].rearrange("p (a b) -> p a b", a=cntk),
                                    in1=Emh[:, h * 3 + g:h * 3 + g + 1, 0:ncol].broadcast_to([128, cntk, ncol]),
                                    op=ALU.mult), reads=[f"PT{pt}", emr], writes=[f"PT{pt}"])
                        return pt

                    def back(pt, pack=pack, g=g):
                        bks = set()
                        for kb, ncol, off in pack:
                            for qi in range(ncol // 128):
                                for (bk, c0, cs, nc_, so) in qblock_dst(g, kb + qi):
                                    bks.add(bk)

                        def mmpv(e):
                            r = None
                            for kb, ncol, off in pack:
                                if g == 0 and ncol == 256 and kb % 4 != 3:
                                    bk = kb // 4
                                    c0 = (kb % 4) * 128
                                    r = e.matmul(banks[bk][:, c0:c0 + 256], lhsT=Vt[g][:, kb, 64 * h:64 * h + 128],
                                                 rhs=PT[pt][:, off:off + 256], start=(not started[bk]), stop=False,
                                                 skip_group_check=True)
                                    started[bk] = True
                                    continue
                                for qi in range(ncol // 128):
                                    for (bk, c0, cs, nc_, so) in qblock_dst(g, kb + qi):
                                        o = off + qi * 128 + so
                                        r = e.matmul(banks[bk][:, c0:c0 + cs * (nc_ - 1) + 1:cs],
                                                     lhsT=Vt[g][:, kb, 64 * h:64 * h + 128],
                                                     rhs=PT[pt][:, o:o + nc_], start=(not started[bk]), stop=False,
                                                     skip_group_check=True)
                                        started[bk] = True
                            return r
                        vregs = sorted({f"V{g}_{kb // 4}a" for kb, _, _ in pack} | {f"V{g}_{kb // 4}b" for kb, _, _ in pack})
                        P.op("pe", mmpv, reads=[f"PT{pt}", f"Vones{g}"] + vregs, writes=[BK[b_] for b_ in sorted(bks)])
                    items.append((front, back))
            return items

        def attn_evac(hp, h):
            nlo, dlo = (0, 64) if h == 0 else (64, 0)
            rdb = [(rd, "rdA"), (sdt[0], "sdt0")]
            otb = [(ot, "otA"), (sdt[1], "sdt1")]
            for bk in range(4):
                cs_ = slice(bk * 512, (bk + 1) * 512)
                rdx, rdn = rdb[bk % 2]
                otx, otn = otb[bk % 2]
                P.op("act", lambda e, bk=bk, rdx=rdx: e.activation(out=rdx[nlo:nlo + 64, :], in_=banks[bk][dlo:dlo + 64, :],
                                                                   func=AF.Ln), reads=[BK[bk]], writes=[rdn])
                P.op("act", lambda e, rdx=rdx: e.activation(out=rdx[nlo:nlo + 64, :], in_=rdx[nlo:nlo + 64, :], func=AF.Exp,
                                                            scale=-1.0), reads=[rdn], writes=[rdn])
                P.op("dve", lambda e, bk=bk, rdx=rdx, otx=otx: e.tensor_tensor(
                    out=otx[nlo:nlo + 64, :], in0=banks[bk][nlo:nlo + 64, :], in1=rdx[nlo:nlo + 64, :], op=ALU.mult),
                    reads=[BK[bk], rdn], writes=[otn])
                P.op("pool", lambda e, cs_=cs_, otx=otx: e.tensor_tensor(
                    out=uT[nlo:nlo + 64, hp, cs_], in0=otx[nlo:nlo + 64, :], in1=sz[nlo:nlo + 64, cs_], op=ALU.mult),
                    reads=[otn, f"sz{bk}"], writes=[f"uT{hp}_{h}_{bk}"])

        prefetch(0)
        for hp in range(8):
            cnt["p"] = 0
            run_items(proj_items(hp), look=1)
            maybe_stop("p1")
            if hp + 1 < 8:
                prefetch(hp + 1)
            its = []
            for h in range(2):
                its += attn_items(hp, h)
                its.append(("evac", (lambda hp=hp, h=h: attn_evac(hp, h))))
            run_items(its, group=2)
            maybe_stop("a2")
        P.barrier()
        arena.reset(mark)
        phase_out(1, wo1_d, uT, lambda tt: [f"uT{kc}_{h}_{tt // 4}" for kc in range(8) for h in range(2)],
                  is_last=(last and layers[-1] == 1))

    if not EARLY_NORM:
        emit_mod(layers[0])
    try:
        maybe_stop("setup")
        for l in layers:
            if l == 0:
                layer0()
            else:
                layer1()
    except _Stop:
        pass
    need = {}
    for tt in range(NT):
        for k, v in P.rd.get(f"x{tt}", {}).items():
            need[k] = max(need.get(k, 0), v)
    P._waits("sp", need)
    P.emit()
    return nc, arena.peak


def _pp(v):
    return np.ascontiguousarray(np.asarray(v, np.float32).reshape(8, 128).T)


def _prep_shared(inp):
    f = lambda k: np.asarray(inp[k], np.float32)
    ada_b = f("ada_b")
    vecs = np.zeros((128, NVEC), np.float32)
    ng = f("norm_g")
    for l in range(2):
        vecs[:, V_NG[l]:V_NG[l] + 8] = _pp(ng[l])
        vecs[:, V_SHB[l]:V_SHB[l] + 8] = _pp(ada_b[l, 0:D])
        vecs[:, V_SCB[l]:V_SCB[l] + 8] = _pp(ada_b[l, D:2 * D])
    vecs[:, V_CONVB:V_CONVB + 8] = _pp(f("a_conv_b")[0])
    vecs[:, V_LNG:V_LNG + 8] = _pp(f("a_ln_g")[0])
    vecs[:, V_LNB:V_LNB + 8] = _pp(f("a_ln_b")[0])
    cw = f("a_conv_w")[0]
    vecs[:, V_CONVW:V_CONVW + 8 * KW] = cw.reshape(KW, 8, 128).transpose(2, 1, 0).reshape(128, 8 * KW)
    qn = f("b_q_norm")[0]
    kn = f("b_k_norm")[0]
    vecs[:, V_QG:V_QG + 3] = np.concatenate([qn.T, qn.T], axis=0)
    vecs[:, V_KG:V_KG + 3] = np.concatenate([kn.T, kn.T], axis=0)
    gateb = np.ascontiguousarray(ada_b[:, 2 * D:3 * D].reshape(1, 2 * D))
    w_in0 = f("a_w_in")[0]
    w0 = np.ascontiguousarray(w_in0.reshape(D, 3, 8, 128).transpose(2, 0, 1, 3).reshape(8, D, 384))
    w_in1 = f("b_w_in")[0]
    qkv = w_in1[:, :9216].reshape(D, 3, 3, 8, 128)
    zz = w_in1[:, 9216:].reshape(D, 8, 128)
    w1 = np.concatenate([qkv.transpose(3, 0, 1, 2, 4).reshape(8, D, 1152), zz.transpose(1, 0, 2)], axis=2)
    w1 = np.ascontiguousarray(w1)
    cst = np.zeros((128, 256), np.float32)
    cst[:, 0:128] = np.eye(128, dtype=np.float32)
    cst[0:64, 128:192] = 1.0
    cst[64:128, 192:256] = 1.0
    r = np.arange(128)[:, None].astype(np.float64)
    c = np.arange(256)[None, :].astype(np.float64)
    steps = c - r
    valid = (steps >= 0) & (steps <= 128)
    emask = np.zeros((8, 128, 6, 256), np.float32)
    for hp in range(8):
        for hl in range(2):
            hh = 2 * hp + hl
            slope = 2.0 ** (-8.0 * (hh + 1) / 16)
            for g in range(3):
                emask[hp, :, hl * 3 + g, :] = np.where(valid, np.exp(-slope * DILS[g] * steps), 0.0)
    return dict(vecs=vecs, gateb=gateb, ada_w=np.ascontiguousarray(f("ada_w")), w0=w0,
                wo0=np.ascontiguousarray(f("a_w_out")[0]), w1=w1, wo1=np.ascontiguousarray(f("b_w_out")[0]),
                emask=np.ascontiguousarray(emask.reshape(8, 128, 6 * 256)), cst=cst)


FUSED = True
_cache = {}


def _get_nc(layers, first, last):
    key = (tuple(layers), first, last)
    if key not in _cache:
        _cache[key] = build(list(layers), first, last)[0]
    return _cache[key]


def _run(layers, xin, shared, c):
    nc = _get_nc(layers, True, True)
    in_maps = []
    for b in range(8):
        m = dict(shared)
        m["x"] = np.ascontiguousarray(xin[b])
        m["cT"] = _pp(c[b])
        in_maps.append(m)
    res = run_bass_kernel_spmd(nc, in_maps, core_ids=list(range(8)))
    return np.stack([res.results[b]["y"] for b in range(8)], axis=0)


def kernel(**inputs):
    x = np.asarray(inputs["x"], np.float32)
    c = np.asarray(inputs["c"], np.float32)
    shared = _prep_shared(inputs)
    if FUSED:
        return _run((0, 1), x, shared, c).astype(np.float32)
    x1 = _run((0,), x, shared, c)
    return _run((1,), x1, shared, c).astype(np.float32)
```
